# Optimizing a Trainium2 kernel written in Bass

```python
import jax, jax.numpy as jnp
from jax import lax
import numpy as np

D_MODEL = 1024
BATCH = 8
SEQ = 2048
DEPTH = 1
DEC_BATCH = 32
DEC_SEQ = 32
PAST_LEN = 4096

CHUNK = 64
D_CONV = D_MODEL
D_POOL = D_MODEL
CONV_W = 3
POOL_WINDOWS = (2, 4, 8, 16)
N_POOL_GROUPS = len(POOL_WINDOWS)
POOL_GW = D_POOL // N_POOL_GROUPS
POOL_HIST = max(POOL_WINDOWS) - 1
PLE_DIM = 256
N_IN_COLS = 4 * D_CONV + 2 * D_POOL + 2 * D_MODEL
EPS = 1e-6

kernel_name = "gated_conv_pool_streaming_encoder_step"


def _rmsnorm(x, g):
    xf = x.astype(jnp.float32)
    y = xf * lax.rsqrt(jnp.mean(xf * xf, axis=-1, keepdims=True) + EPS)
    return (y * g.astype(jnp.float32)).astype(x.dtype)


def _layer(x, p, conv_hist, pool_hist, offset, g_pre, w_in, conv_w, conv_b, w_grp,
           pool_scale, w_a_out, w_b_out, w_o, g_post, w_ple, w_pgate):
    b, L, _ = x.shape
    h = _rmsnorm(x, g_pre)
    proj = h @ w_in
    sizes = [D_CONV, D_CONV, D_CONV, D_CONV, D_POOL, D_POOL, D_MODEL, D_MODEL]
    cuts = list(np.cumsum(sizes)[:-1])
    xa, ba, ca, za, xb, zb, ga, gb = jnp.split(proj, cuts, axis=-1)

    u = ca * xa
    up = jnp.concatenate([conv_hist.astype(u.dtype), u], axis=1)
    conv = conv_b + sum(conv_w[k] * up[:, k:k + L] for k in range(CONV_W))
    ya = ((ba * conv) * jax.nn.silu(za)) @ w_a_out
    new_conv = up[:, -(CONV_W - 1):]

    bp = jnp.concatenate([pool_hist.astype(xb.dtype), xb], axis=1)
    cs = jnp.cumsum(bp.astype(jnp.float32), axis=1)
    cs0 = jnp.concatenate([jnp.zeros((b, 1, D_POOL), jnp.float32), cs], axis=1)
    pos = offset + jnp.arange(L)
    start = POOL_HIST + 1
    pooled = []
    for g, w in enumerate(POOL_WINDOWS):
        sl = slice(g * POOL_GW, (g + 1) * POOL_GW)
        s = cs0[:, start:, sl] - cs0[:, start - w:start - w + L, sl]
        cnt = jnp.minimum(pos + 1, w).astype(jnp.float32)
        pooled.append(s / cnt[None, :, None])
    pooled = jnp.concatenate(pooled, axis=-1)
    d = (pooled - xb.astype(jnp.float32)).astype(xb.dtype).reshape(b, L, N_POOL_GROUPS, POOL_GW)
    mixed = jnp.einsum('blgc,gcd->blgd', d, w_grp).reshape(b, L, D_POOL) * pool_scale
    yb = (mixed * jax.nn.silu(zb)) @ w_b_out
    new_pool = bp[:, -POOL_HIST:]

    m = jax.nn.sigmoid(ga) * ya + jax.nn.sigmoid(gb) * yb
    o = m @ w_o
    x1 = x + _rmsnorm(o, g_post)

    e = (p @ w_ple) * jax.nn.sigmoid(x1 @ w_pgate)
    return x1 + e, new_conv, new_pool


def setup_inputs(seed: int = 0) -> dict:
    key = jax.random.key(seed)
    ks = jax.random.split(key, 20)
    f32 = jnp.float32
    nrm = lambda k, shape, scale: jax.random.normal(k, shape, f32) * scale
    return {
        "x_prompt": nrm(ks[0], (BATCH, SEQ, D_MODEL), 1.0),
        "x_sample": nrm(ks[1], (DEC_BATCH, DEC_SEQ, D_MODEL), 1.0),
        "p_prompt": nrm(ks[2], (DEPTH, BATCH, SEQ, PLE_DIM), 1.0),
        "p_sample": nrm(ks[3], (DEPTH, DEC_BATCH, DEC_SEQ, PLE_DIM), 1.0),
        "cache_conv": nrm(ks[4], (DEPTH, DEC_BATCH, CONV_W - 1, D_CONV), 1.0),
        "state_pool": nrm(ks[5], (DEPTH, DEC_BATCH, POOL_HIST, D_POOL), 1.0),
        "g_pre": 1.0 + nrm(ks[6], (DEPTH, D_MODEL), 0.05),
        "w_in": nrm(ks[7], (DEPTH, D_MODEL, N_IN_COLS), D_MODEL ** -0.5),
        "conv_w": nrm(ks[8], (DEPTH, CONV_W, D_CONV), CONV_W ** -0.5),
        "conv_b": nrm(ks[9], (DEPTH, D_CONV), 0.02),
        "w_grp": nrm(ks[10], (DEPTH, N_POOL_GROUPS, POOL_GW, POOL_GW), POOL_GW ** -0.5),
        "pool_scale": 1.0 + nrm(ks[11], (DEPTH, D_POOL), 0.05),
        "w_a_out": nrm(ks[12], (DEPTH, D_CONV, D_MODEL), D_CONV ** -0.5),
        "w_b_out": nrm(ks[13], (DEPTH, D_POOL, D_MODEL), D_POOL ** -0.5),
        "w_o": nrm(ks[14], (DEPTH, D_MODEL, D_MODEL), D_MODEL ** -0.5),
        "g_post": 1.0 + nrm(ks[15], (DEPTH, D_MODEL), 0.05),
        "w_ple": nrm(ks[16], (DEPTH, PLE_DIM, D_MODEL), PLE_DIM ** -0.5),
        "w_pgate": nrm(ks[17], (DEPTH, D_MODEL, D_MODEL), D_MODEL ** -0.5),
    }


def reference(x_prompt, x_sample, p_prompt, p_sample, cache_conv, state_pool,
              g_pre, w_in, conv_w, conv_b, w_grp, pool_scale, w_a_out, w_b_out,
              w_o, g_post, w_ple, w_pgate):
    hp = x_prompt
    hs = x_sample
    conv_p, conv_s, pool_p, pool_s = [], [], [], []
    bp = x_prompt.shape[0]
    for i in range(DEPTH):
        lw = (g_pre[i], w_in[i], conv_w[i], conv_b[i], w_grp[i], pool_scale[i],
              w_a_out[i], w_b_out[i], w_o[i], g_post[i], w_ple[i], w_pgate[i])
        zc = jnp.zeros((bp, CONV_W - 1, D_CONV), hp.dtype)
        zp = jnp.zeros((bp, POOL_HIST, D_POOL), hp.dtype)
        hp, nc_p, np_p = _layer(hp, p_prompt[i], zc, zp, 0, *lw)
        hs, nc_s, np_s = _layer(hs, p_sample[i], cache_conv[i], state_pool[i], PAST_LEN, *lw)
        conv_p.append(nc_p); pool_p.append(np_p)
        conv_s.append(nc_s); pool_s.append(np_s)
    new_conv_prompt = jnp.stack(conv_p)
    new_conv_sample = jnp.stack(conv_s)
    new_pool_prompt = jnp.stack(pool_p)
    new_pool_sample = jnp.stack(pool_s)
    return (hp, hs, new_conv_prompt, new_conv_sample, new_pool_prompt, new_pool_sample)
```

```python
from contextlib import ExitStack

import numpy as np
import concourse.bass as bass
import concourse.mybir as mybir
from concourse.bass_utils import run_bass_kernel_spmd

F32 = mybir.dt.float32
BF16 = mybir.dt.bfloat16
AF = mybir.ActivationFunctionType
ALU = mybir.AluOpType
ENGS = ('pe', 'act', 'dve', 'pool', 'sp')

N_CORES = 8
D = 1024
EPS = 1e-6
WINS = (2, 4, 8, 16)
S_XA, S_BA, S_CA, S_ZA, S_XB, S_ZB, S_GA, S_GB = range(8)

SBS = [
    [(0, 512), (512, 512)],
    [(1024, 512), (1536, 512), None],
]
MAXP = 1024
NT = MAXP + 128
SOFF = 16 + MAXP
UW = SOFF + 4 * 48
N_SLAB = 2
NW = 7
TBW = 528
P0_PERIOD = 2.0
P3_PERIOD = 9.5


class Res:
    __slots__ = ('name', 'lw', 'rs', 'pending', 'epoch')

    def __init__(self, name):
        self.name = name
        self.lw = None
        self.rs = {}
        self.pending = 0
        self.epoch = None


class Prog:
    def __init__(self):
        self.ops = {e: [] for e in ENGS}
        self.cnt = {}
        self.known = {e: {} for e in ENGS}

    def op(self, eng, emit, reads=(), writes=(), dma=None, n_dma=1):
        if dma is None:
            clk, inc = eng, 1
        else:
            clk, inc = dma, 16 * n_dma
        deps = {}

        def add(d, raw):
            if d is None:
                return
            c, v = d
            if deps.get(c, 0) < v:
                deps[c] = v
        for r in reads:
            add(r.lw, True)
            if r.pending > 0:
                r.pending -= 1
        for w in writes:
            add(w.lw, False)
            for c, v in w.rs.items():
                add((c, v), False)
        waits = []
        kn = self.known[eng]
        for c, v in deps.items():
            if kn.get(c, 0) >= v:
                continue
            kn[c] = v
            waits.append((c, v))
        self.cnt[clk] = self.cnt.get(clk, 0) + inc
        seq = self.cnt[clk]
        for w in writes:
            w.lw = (clk, seq)
            w.rs = {}
        for r in reads:
            if r.rs.get(clk, 0) < seq:
                r.rs[clk] = seq
        self.ops[eng].append((waits, emit, clk, 16 if dma is not None else 1))

    def finish(self, eng='sp'):
        waits = [(c, v) for c, v in self.cnt.items() if c != eng]
        self.ops[eng].append((waits, None, None, 0))

    def emit_all(self, nc, stack):
        sems = {}
        for c in self.cnt:
            sems[c] = stack.enter_context(nc.semaphore("s_" + c))
        block = stack.enter_context(nc.Block())
        names = {'pe': 'tensor', 'act': 'scalar', 'dve': 'vector', 'pool': 'gpsimd', 'sp': 'sync'}
        for eng in ENGS:
            ops = self.ops[eng]

            def body(e, ops=ops):
                for waits, emit, clk, inc in ops:
                    for c, v in waits:
                        e.wait_ge(sems[c], v)
                    if emit is None:
                        continue
                    sem = sems[clk]
                    emit(e, lambda ins, sem=sem, inc=inc: ins.then_inc(sem, inc))
            getattr(block, names[eng])(body)


class Blk:
    pass


def build_program():
    nc = bass.Bass("TRN2", target_bir_lowering=False)

    def din(name, shape):
        return nc.dram_tensor(name, shape, F32, kind="ExternalInput").ap()

    def dout(name, shape):
        return nc.dram_tensor(name, shape, F32, kind="ExternalOutput").ap()

    xp = din("xp", [2048, D])
    xs = din("xs", [128, D])
    pp = din("pp", [2048, 256])
    psm = din("ps", [128, 256])
    ccd = din("cc", [8, D])
    spd = din("sp", [60, D])
    g_pre = din("g_pre", [1, D])
    w_in = din("w_in", [64 * 128, D])
    conv_w = din("conv_w", [3, D])
    conv_b = din("conv_b", [1, D])
    w_grp = din("w_grp", [1024, 256])
    pool_scale = din("pool_scale", [1, D])
    w_a_out = din("w_a_out", [8 * 128, D])
    w_b_out = din("w_b_out", [8 * 128, D])
    w_o = din("w_o", [D, D])
    g_post = din("g_post", [1, D])
    w_ple = din("w_ple", [256, D])
    w_pgate = din("w_pgate", [D, D])
    yp = dout("yp", [2048, D])
    ys = dout("ys", [128, D])
    ncp = dout("ncp", [2, D])
    ncs = dout("ncs", [8, D])
    npp = dout("npp", [15, D])
    nps = dout("nps", [60, D])

    P = Prog()
    OP = P.op
    st = ExitStack()
    with st:
        def sb(name, shape, dt):
            return st.enter_context(nc.sbuf_tensor(name, shape, dt))

        ident_f = sb("ident_f", [128, 128], F32)
        ident_b = sb("ident_b", [128, 128], BF16)
        smallT = sb("smallT", [128, 8, 73], F32)
        gpre_bc = sb("gpre_bc", [128, D], F32)
        gpost_bc = sb("gpost_bc", [128, D], F32)
        invcnt = sb("invcnt", [128, 4, 16], F32)
        mhalf = sb("mhalf", [128, 1], F32)
        epsc = sb("epsc", [128, 1], F32)
        stats = sb("stats", [128, 6, 18], F32)
        hb = [sb(f"hb{i}", [128, D], BF16) for i in range(3)]
        x1b = [sb(f"x1b{i}", [128, D], BF16) for i in range(2)]
        pb = [sb(f"pb{i}", [128, 256], BF16) for i in range(2)]
        pTt = [sb(f"pTt{i}", [128, 2, 128], BF16) for i in range(3)]
        x1T = [sb(f"x1T{i}", [128, 8, 128], BF16) for i in range(1)]
        hT = sb("hT", [128, 8, NT], BF16)
        vT = sb("vT", [128, 8, NT], BF16)
        qT = sb("qT", [128, 8, NT], BF16)
        mT = sb("mT", [128, 8, NT], BF16)
        U = sb("U", [128, UW], F32)
        XB = sb("XB", [128, UW], F32)
        tail_u = sb("tail_u", [128, 8, 10], F32)
        tail_x = sb("tail_x", [128, 8, 75], F32)
        slabs = [sb(f"slab{i}", [128, 6, 8, 128], BF16) for i in range(N_SLAB)]
        wo_sb = sb("wo_sb", [128, 8, D], BF16)
        wpg_sb = sb("wpg_sb", [128, 8, D], BF16)
        wple_sb = sb("wple_sb", [128, 2, D], BF16)
        wgrp_sb = sb("wgrp_sb", [128, 4, 2, 256], BF16)
        arena = sb("arena", [128, NW * 2 * TBW], F32)
        ps = [st.enter_context(nc.psum_tensor(f"ps{q}", [128, 1024], F32)) for q in range(4)]

        R = {}

        def res(name):
            if name not in R:
                R[name] = Res(name)
            return R[name]

        pres = [res(f"bank{b}") for b in range(8)]
        tres = [res(f"T{i}") for i in range(2 * NW)]
        slab_res = [[res(f"slab{i}_{si}") for si in range(6)] for i in range(N_SLAB)]

        def bank(b):
            return ps[b // 2][:, (b % 2) * 512:(b % 2) * 512 + 512]

        def bank_bf(b):
            return ps[b // 2][:].bitcast(BF16)[:, (b % 2) * 1024:(b % 2) * 1024 + 1024]

        def tap(i):
            return arena[:, i * TBW:(i + 1) * TBW]

        def wap(w):
            return arena[:, w * 2 * TBW:w * 2 * TBW + 1024]

        def wres(w):
            return [tres[2 * w], tres[2 * w + 1]]

        rot = {}

        def alloc(role, idxs):
            i = rot.get(role, 0)
            rot[role] = i + 1
            return idxs[i % len(idxs)]

        def claim(r, nreads):
            assert r.pending == 0, f"liveness violation on {r.name}: {r.pending} readers not yet emitted"
            r.pending = nreads

        bank_state = {'singles': list(range(8))}

        def balloc(nreads=1, idxs=None):
            idxs = bank_state['singles'] if idxs is None else idxs
            b = alloc('bank:' + ','.join(map(str, idxs)), idxs)
            claim(pres[b], nreads)
            return b

        def palloc(nreads=2):
            q = alloc('pair', [0, 1])
            claim(pres[2 * q], nreads)
            claim(pres[2 * q + 1], nreads)
            return q

        stage_w = 6
        stage = wap(stage_w)
        OP('pool', lambda e, d: d(e.memset(tap(11)[:, 0:128], 1.0)), writes=[tres[11]])
        OP('pool', lambda e, d: d(e.affine_select(out=ident_f[:], in_=tap(11)[:, 0:128], pattern=[[-1, 128]],
                                                   compare_op=ALU.is_equal, fill=0.0, base=0, channel_multiplier=1)),
           reads=[tres[11]], writes=[res("ident_f")])
        def f_consts(e, d):
            e.memset(epsc[:], EPS)
            d(e.memset(mhalf[:], -0.5))
        OP('pool', f_consts, writes=[res("mhalf")])

        def f_invcnt(e, d):
            ins = None
            for g, W in enumerate(WINS):
                ins = e.memset(invcnt[:, g, W - 1:16], 1.0 / W)
                for t in range(W - 1):
                    ins = e.memset(invcnt[:, g, t:t + 1], 1.0 / (t + 1))
            d(ins)
        OP('pool', f_invcnt, writes=[res("invcnt")])

        def f_zero(e, d):
            e.memset(U[:, 0:16], 0.0)
            e.memset(XB[:, 0:16], 0.0)
            d(e.memset(stats[:], 0.0))
        OP('pool', f_zero, writes=[res("U0"), res("XB0"), res("stats")])

        OP('dve', lambda e, d: d(e.tensor_copy(out=ident_b[:], in_=ident_f[:])), reads=[res("ident_f")], writes=[res("ident_b")])

        def setup_stage_load():
            def f_stage(e, d):
                d(e.dma_start(out=stage[0:3, :], in_=conv_w[:, :]))
                d(e.dma_start(out=stage[3:4, :], in_=conv_b[:, :]))
                d(e.dma_start(out=stage[4:5, :], in_=pool_scale[:, :]))
                d(e.dma_start(out=stage[5:13, :], in_=ccd[:, :]))
                d(e.dma_start(out=stage[13:73, :], in_=spd[:, :]))
            OP('sp', f_stage, writes=wres(stage_w), dma=f"ldW{stage_w}", n_dma=5)

        def setup_gpre():
            OP('sp', lambda e, d: d(e.dma_start(out=gpre_bc[:], in_=g_pre[0:1, :].partition_broadcast(128))), writes=[res("gpre")], dma="ld_gpre")

        def setup_gpost():
            OP('sp', lambda e, d: d(e.dma_start(out=gpost_bc[:], in_=g_post[0:1, :].partition_broadcast(128))), writes=[res("gpost")], dma="ld_gpost")

        def setup_stage_transpose():
            def f_tr_stage(e, d):
                ins = None
                for j in range(8):
                    q = ps[j // 4]
                    ins = e.transpose(out=q[:, (j % 4) * 73:(j % 4) * 73 + 73], in_=stage[0:73, j * 128:(j + 1) * 128],
                                      identity=ident_f[0:73, 0:73])
                d(ins)
            claim(pres[0], 1)
            claim(pres[2], 1)
            OP('pe', f_tr_stage, reads=wres(stage_w) + [res("ident_f")], writes=[pres[0], pres[2]])

            def f_cp_stage(e, d):
                e.tensor_copy(out=smallT[:, 0:4, :], in_=ps[0][:, 0:292].rearrange("p (j c) -> p j c", j=4))
                d(e.tensor_copy(out=smallT[:, 4:8, :], in_=ps[1][:, 0:292].rearrange("p (j c) -> p j c", j=4)))
            OP('dve', f_cp_stage, reads=[pres[0], pres[2]], writes=[res("smallT")])

        OP('act', lambda e, d: d(e.activation(out=stats[:, 5, 17:18], in_=mhalf[:, 0:1], func=AF.Square)),
           reads=[res("mhalf"), res("stats")], writes=[res("rstd2_dummy")])

        slab_seq = []
        for sbi in range(len(SBS)):
            for ph in (1, 2):
                for j in range(8):
                    slab_seq.append((sbi, ph, j))
        slab_next = [0]

        def issue_slab_load(part=None):
            idx = slab_next[0]
            if idx >= len(slab_seq):
                return
            if part != 'head':
                slab_next[0] += 1
            sbi, ph, j = slab_seq[idx]
            slot = idx % N_SLAB
            sl = slabs[slot]
            def blk(t, b):
                return t[b * 128:(b + 1) * 128, :].rearrange("p (k c) -> p k c", k=8)
            if ph == 1:
                srcs = [blk(w_in, sec * 8 + j) for sec in (S_XA, S_BA, S_CA, S_ZA, S_XB, S_ZB)]
            else:
                srcs = [blk(w_in, S_GA * 8 + j), blk(w_in, S_GB * 8 + j), blk(w_a_out, j), blk(w_b_out, j)]

            order = (0, 2, 1, 3, 4, 5) if ph == 1 else (0, 1, 2, 3)
            if part == 'head':
                order = order[:1]
            elif part == 'rest':
                order = order[1:]
            for si in order:
                src = srcs[si]
                OP('pool', lambda e, d, sl=sl, si=si, src=src: d(e.dma_start(out=sl[:, si, :, :], in_=src)),
                   writes=[slab_res[slot][si]], dma=f"slab{slot}_{si}")

        def slab_of(sbi, ph, j):
            idx = slab_seq.index((sbi, ph, j))
            assert idx < slab_next[0], "slab not issued yet"
            return slabs[idx % N_SLAB], slab_res[idx % N_SLAB]

        def load_wgrp():
            OP('pool', lambda e, d: d(e.dma_start(out=wgrp_sb[:], in_=w_grp.rearrange("(g k p) c -> p g k c", g=4, k=2, p=128))),
               writes=[res("wgrp")], dma="ld_wgrp")

        resident_loads = []
        for name, dst, src, nk in (("wo", wo_sb, w_o, 8), ("wpg", wpg_sb, w_pgate, 8), ("wple", wple_sb, w_ple, 2)):
            for kk in range(nk):
                def ld(name=name, dst=dst, src=src, kk=kk):
                    OP('pool', lambda e, d: d(e.dma_start(out=dst[:, kk, :], in_=src[kk * 128:(kk + 1) * 128, :])),
                       writes=[res(name)], dma="ld_" + name)
                resident_loads.append(ld)


        sb_info = []
        for sbi, blks in enumerate(SBS):
            info = Blk()
            info.sbi = sbi
            info.blocks = []
            rel = 0
            info.p0 = None
            for bi, b in enumerate(blks):
                k = Blk()
                k.idx = bi
                if b is not None:
                    k.samp = False
                    k.t0g, k.N = b
                    if info.p0 is None:
                        info.p0 = k.t0g
                    k.rel = k.t0g - info.p0
                else:
                    k.samp = True
                    k.N = 128
                    k.rel = rel
                rel = k.rel + k.N
                info.blocks.append(k)
            info.Lp = sum(k.N for k in info.blocks if not k.samp)
            info.has_samp = any(k.samp for k in info.blocks)
            info.tiles = []
            for k in info.blocks:
                for t in range(k.N // 128):
                    tl = Blk()
                    tl.sbi = sbi
                    tl.rel = k.rel + t * 128
                    tl.blk = k
                    if k.samp:
                        tl.gi = 16
                        tl.x = xs[:, :]
                        tl.p = psm[:, :]
                        tl.y = ys[:, :]
                    else:
                        r0 = k.t0g + t * 128
                        tl.gi = r0 // 128
                        tl.x = xp[r0:r0 + 128, :]
                        tl.p = pp[r0:r0 + 128, :]
                        tl.y = yp[r0:r0 + 128, :]
                    info.tiles.append(tl)
            sb_info.append(info)

        def ares(name, j, k):
            return res(f"{name}_{j}_{k.idx}")

        def hres(k, sbi=None):
            rr = [res(f"hT_t{t}") for t in range(k.rel // 128, (k.rel + k.N) // 128)]
            if sbi is not None:
                for r in rr:
                    assert r.epoch == sbi, f"{r.name} read for SB{sbi} before its head wrote it (epoch {r.epoch})"
            return rr

        def hres_tile(tl):
            return res(f"hT_t{tl.rel // 128}")

        def stat(row, gi):
            return stats[:, row, gi:gi + 1]

        def run_timed(items):
            items = sorted(enumerate(items), key=lambda t: (t[1][0], t[0]))
            for _, (_, fn) in items:
                fn()

        def phase0_ops(tl, base, tr_bank=None, xt_idx=(0, 1, 2)):
            gi = tl.gi
            s = Blk()
            ssr = res(f"ss_{gi}")

            def a():
                s.w = alloc('xt' if len(xt_idx) == 3 else 'xt_head', list(xt_idx))
                claim(wres(s.w)[0], 2)
                OP('sp', lambda e, d: d(e.dma_start(out=wap(s.w), in_=tl.x)), writes=wres(s.w), dma=f"ldW{s.w}")

            def b():
                s.hi = alloc('hb', [0, 1, 2])
                claim(res(f"hb{s.hi}"), 1)
                OP('act', lambda e, d: d(e.activation(out=hb[s.hi][:], in_=wap(s.w), func=AF.Square, accum_out=stat(0, gi))),
                   reads=wres(s.w) + [res("stats")], writes=[res(f"hb{s.hi}"), ssr])

            def c():
                OP('act', lambda e, d: d(e.activation(out=stat(1, gi), in_=stat(0, gi), func=AF.Identity, bias=epsc[:, 0:1], scale=1.0 / D)),
                   reads=[ssr, res("mhalf")], writes=[res(f"ms_{gi}")])

            def c2():
                OP('pool', lambda e, d: d(e.tensor_tensor(out=stat(2, gi), in0=stat(1, gi), in1=mhalf[:, 0:1], op=ALU.pow)),
                   reads=[res(f"ms_{gi}"), res("mhalf")], writes=[res(f"rstd_{gi}")])

            def dd():
                OP('dve', lambda e, d: d(e.scalar_tensor_tensor(out=hb[s.hi][:], in0=wap(s.w), scalar=stat(2, gi), in1=gpre_bc[:],
                                                                 op0=ALU.mult, op1=ALU.mult)),
                   reads=wres(s.w) + [res(f"rstd_{gi}"), res("gpre")], writes=[res(f"hb{s.hi}")])

            def ee():
                s.b = balloc() if tr_bank is None else balloc(1, [tr_bank])

                def f_tr(e, d):
                    ins = None
                    for k in range(8):
                        ins = e.transpose(out=bank_bf(s.b)[:, k * 128:(k + 1) * 128], in_=hb[s.hi][:, k * 128:(k + 1) * 128], identity=ident_b[:])
                    d(ins)
                OP('pe', f_tr, reads=[res(f"hb{s.hi}"), res("ident_b")], writes=[pres[s.b]])

            def ff():
                if gi % 2 == 0:
                    OP('act', lambda e, d: d(e.copy(out=hT[:, :, tl.rel:tl.rel + 128],
                                                    in_=bank_bf(s.b).rearrange("p (k c) -> p k c", k=8))),
                       reads=[pres[s.b]], writes=[hres_tile(tl)])
                    hres_tile(tl).epoch = tl.sbi
                else:
                    OP('dve', lambda e, d: d(e.tensor_copy(out=hT[:, :, tl.rel:tl.rel + 128],
                                                           in_=bank_bf(s.b).rearrange("p (k c) -> p k c", k=8))),
                       reads=[pres[s.b]], writes=[hres_tile(tl)])
                    hres_tile(tl).epoch = tl.sbi
            return [(base + 0.0, a), (base + 2.3, b), (base + 3.6, c), (base + 4.0, c2), (base + 4.8, dd), (base + 6.4, ee), (base + 7.2, ff)]

        def blkview(ap, k):
            if k.samp:
                return ap[:, 0:128].rearrange("p (s c) -> p s c", s=4)
            return ap[:, 0:k.N]

        def actview(t, j, k):
            if k.samp:
                return t[:, j, k.rel:k.rel + 128].rearrange("p (s c) -> p s c", s=4)
            return t[:, j, k.rel:k.rel + k.N]

        def uwin(buf, k, lo, hi):
            if k.samp:
                return buf[:, SOFF:SOFF + 192].rearrange("p (s c) -> p s c", s=4)[:, :, lo:hi]
            return buf[:, k.rel + lo:k.rel + hi]

        def twin(i, k, lo, hi):
            if k.samp:
                return tap(i)[:, 0:192].rearrange("p (s c) -> p s c", s=4)[:, :, lo:hi]
            return tap(i)[:, lo:hi]

        def ures(name, k):
            return res(f"{name}{k.idx}")

        pending_b2 = []

        def phase1_pe_items(k, sl, slr, t0, dt, bidx):
            N = k.N
            banks = []
            items = []
            for g, sec in enumerate((0, 2, 1, 3, 4, 5)):
                def fn(sec=sec):
                    bi = balloc(1, bidx)
                    banks.append(bi)

                    def f(e, d):
                        ins = None
                        for kk in range(8):
                            ins = e.matmul(bank(bi)[:, 0:N], lhsT=sl[:, sec, kk, :], rhs=hT[:, kk, k.rel:k.rel + N],
                                           start=(kk == 0), stop=(kk == 7))
                        d(ins)
                    OP('pe', f, reads=[slr[sec]] + hres(k, 0), writes=[pres[bi]])
                items.append((t0 + dt * g, fn))
            return items, banks

        def phase1_unit(sbi, info, j, k, sl, slr, pre_banks=None):
            N = k.N
            E = 48 if k.samp else 16 + N
            jp, par = j // 2, j % 2
            W = WINS[jp]
            first_prompt = (not k.samp) and k.rel == 0
            last_prompt = (not k.samp) and (k.rel + N == info.Lp)
            if pre_banks is not None:
                banks = pre_banks
            else:
                banks = [balloc() for _ in range(6)]
                for bi, sec in zip(banks, (0, 2, 1, 3, 4, 5)):
                    def f(e, d, bi=bi, sec=sec):
                        ins = None
                        for kk in range(8):
                            ins = e.matmul(bank(bi)[:, 0:N], lhsT=sl[:, sec, kk, :], rhs=hT[:, kk, k.rel:k.rel + N],
                                           start=(kk == 0), stop=(kk == 7))
                        d(ins)
                    OP('pe', f, reads=[slr[sec]] + hres(k, sbi), writes=[pres[bi]])
            b_xa, b_ca, b_ba, b_za, b_xb, b_zb = banks
            cw = lambda t: smallT[:, j, t:t + 1]
            cbias = smallT[:, j, 3:4]
            psc = smallT[:, j, 4:5]
            sm = res("smallT")
            ur = ures("U", k)
            xr = ures("XB", k)
            prev_u = [ures("U", info.blocks[k.idx - 1])] if (not k.samp and k.idx > 0) else []
            prev_x = [ures("XB", info.blocks[k.idx - 1])] if (not k.samp and k.idx > 0) else []
            if first_prompt and sbi > 0:
                OP('dve', lambda e, d: d(e.tensor_copy(out=U[:, 14:16], in_=tail_u[:, j, 0:2])),
                   reads=[res(f"tail_u{j}")], writes=[ur])
                OP('pool', lambda e, d: d(e.tensor_copy(out=XB[:, 1:16], in_=tail_x[:, j, 0:15])),
                   reads=[res(f"tail_x{j}")], writes=[xr])
            if k.samp:
                OP('dve', lambda e, d: d(e.tensor_copy(out=uwin(U, k, 14, 16),
                                                        in_=smallT[:, j, 5:13].rearrange("p (s c) -> p s c", s=4))),
                   reads=[sm], writes=[ur])
                OP('pool', lambda e, d: d(e.tensor_copy(out=uwin(XB, k, 1, 16),
                                                         in_=smallT[:, j, 13:73].rearrange("p (s c) -> p s c", s=4))),
                   reads=[sm], writes=[xr])
            t_xa = alloc('xa', [0, 1])
            OP('act', lambda e, d: d(e.copy(out=blkview(tap(t_xa), k), in_=blkview(bank(b_xa), k))),
               reads=[pres[b_xa]], writes=[tres[t_xa]])
            OP('dve', lambda e, d: d(e.tensor_tensor(out=uwin(U, k, 16, E), in0=blkview(bank(b_ca), k),
                                                      in1=blkview(tap(t_xa), k), op=ALU.mult)),
               reads=[pres[b_ca], tres[t_xa]], writes=[ur])
            t_c = alloc('c', [2, 3])
            cv = blkview(tap(t_c), k)
            OP('act', lambda e, d: d(e.activation(out=cv, in_=uwin(U, k, 16, E), func=AF.Identity, bias=cbias, scale=cw(2))),
               reads=[ur, sm], writes=[tres[t_c]])
            OP('dve', lambda e, d: d(e.scalar_tensor_tensor(out=cv, in0=uwin(U, k, 15, E - 1), scalar=cw(1), in1=cv,
                                                             op0=ALU.mult, op1=ALU.add)),
               reads=[ur, sm, tres[t_c]] + prev_u, writes=[tres[t_c]])
            OP('dve', lambda e, d: d(e.scalar_tensor_tensor(out=cv, in0=uwin(U, k, 14, E - 2), scalar=cw(0), in1=cv,
                                                             op0=ALU.mult, op1=ALU.add)),
               reads=[ur, sm, tres[t_c]] + prev_u, writes=[tres[t_c]])
            t_sz = alloc('sz', [4, 5])
            OP('act', lambda e, d: d(e.activation(out=blkview(tap(t_sz), k), in_=blkview(bank(b_za), k), func=AF.Silu)),
               reads=[pres[b_za]], writes=[tres[t_sz]])
            OP('dve', lambda e, d: d(e.tensor_tensor(out=cv, in0=blkview(bank(b_ba), k), in1=cv, op=ALU.mult)),
               reads=[pres[b_ba], tres[t_c]], writes=[tres[t_c]])
            OP('dve', lambda e, d: d(e.tensor_tensor(out=actview(vT, j, k), in0=cv, in1=blkview(tap(t_sz), k), op=ALU.mult)),
               reads=[tres[t_c], tres[t_sz]], writes=[ares("vT", j, k)])
            OP('act', lambda e, d: d(e.copy(out=uwin(XB, k, 16, E), in_=blkview(bank(b_xb), k))),
               reads=[pres[b_xb]], writes=[xr])
            OP('act', lambda e, d: d(e.activation(out=actview(mT, j, k), in_=blkview(bank(b_zb), k), func=AF.Silu)),
               reads=[pres[b_zb]], writes=[ares("mT", j, k)])
            cur = None
            cur_res = [xr] + prev_x
            lo = 1
            w = 1
            tsel = 0
            t_s = [alloc('sA', [6, 7]), alloc('sB', [8, 9])]
            while w < W:
                nlo = lo + w
                out_i = t_s[tsel]

                def f(e, d, cur=cur, lo=lo, nlo=nlo, w=w, out_i=out_i):
                    if cur is None:
                        a = uwin(XB, k, nlo, E)
                        b_ = uwin(XB, k, nlo - w, E - w)
                    else:
                        a = twin(cur, k, nlo, E)
                        b_ = twin(cur, k, nlo - w, E - w)
                    d(e.tensor_tensor(out=twin(out_i, k, nlo, E), in0=a, in1=b_, op=ALU.add))
                OP('pool', f, reads=cur_res, writes=[tres[out_i]])
                cur = out_i
                cur_res = [tres[out_i]]
                lo = nlo
                w *= 2
                tsel ^= 1
            t_sw = cur
            OP('dve', lambda e, d: d(e.scalar_tensor_tensor(out=actview(qT, j, k), in0=twin(t_sw, k, 16, E), scalar=1.0 / W,
                                                             in1=uwin(XB, k, 16, E), op0=ALU.mult, op1=ALU.subtract)),
               reads=[tres[t_sw], xr], writes=[ares("qT", j, k)])
            if first_prompt and sbi == 0:
                t_fx = 10
                OP('dve', lambda e, d: d(e.tensor_tensor(out=tap(t_fx)[:, 0:16], in0=tap(t_sw)[:, 16:32], in1=invcnt[:, jp, :], op=ALU.mult)),
                   reads=[tres[t_sw], res("invcnt")], writes=[tres[t_fx]])
                OP('dve', lambda e, d: d(e.tensor_tensor(out=qT[:, j, 0:16], in0=tap(t_fx)[:, 0:16], in1=XB[:, 16:32], op=ALU.subtract)),
                   reads=[tres[t_fx], xr], writes=[ares("qT", j, k)])
            if last_prompt:
                Lp = info.Lp
                OP('dve', lambda e, d: d(e.tensor_copy(out=tail_u[:, j, 0:2], in_=U[:, 16 + Lp - 2:16 + Lp])),
                   reads=[ur], writes=[res(f"tail_u{j}")])
                OP('pool', lambda e, d: d(e.tensor_copy(out=tail_x[:, j, 0:15], in_=XB[:, 16 + Lp - 15:16 + Lp])),
                   reads=[xr], writes=[res(f"tail_x{j}")])
            if k.samp:
                OP('dve', lambda e, d: d(e.tensor_copy(out=tail_u[:, j, 2:10].rearrange("p (s c) -> p s c", s=4),
                                                        in_=uwin(U, k, 46, 48))),
                   reads=[ur], writes=[res(f"tail_su{j}")])
                OP('pool', lambda e, d: d(e.tensor_copy(out=tail_x[:, j, 15:75].rearrange("p (s c) -> p s c", s=4),
                                                         in_=uwin(XB, k, 33, 48))),
                   reads=[xr], writes=[res(f"tail_sx{j}")])
            if par == 1:
                def b2(jp=jp, k=k):
                    N_ = k.N
                    for c in range(2):
                        bb = balloc()
                        jj = 2 * jp + c

                        def f(e, d, bb=bb, c=c):
                            ins = None
                            for kk in range(2):
                                ins = e.matmul(bank(bb)[:, 0:N_], lhsT=wgrp_sb[:, jp, kk, c * 128:(c + 1) * 128],
                                               rhs=qT[:, 2 * jp + kk, k.rel:k.rel + N_], start=(kk == 0), stop=(kk == 1))
                            d(ins)
                        OP('pe', f, reads=[res("wgrp"), ares("qT", 2 * jp, k), ares("qT", 2 * jp + 1, k)], writes=[pres[bb]])
                        pending_q.append((bb, jj, k))
                pending_b2.append(b2)

        pending_q = []

        def flush_q():
            while pending_q:
                bb, jj, k = pending_q.pop(0)
                OP('dve', lambda e, d, bb=bb, jj=jj, k=k: d(e.scalar_tensor_tensor(
                    out=actview(qT, jj, k), in0=blkview(bank(bb), k), scalar=smallT[:, jj, 4:5],
                    in1=actview(mT, jj, k), op0=ALU.mult, op1=ALU.mult)),
                   reads=[pres[bb], res("smallT"), ares("mT", jj, k)], writes=[ares("qT", jj, k)])

        def flush_b2(keep=0):
            while len(pending_b2) > keep:
                pending_b2.pop(0)()
                flush_q()

        def phase2_ops(info, j, k, sl, slr, base, bidx=None, tidx=None):
            N = k.N
            sc = N / 512.0
            s = Blk()
            s.b = {}
            ops = []
            srcs = [(0, hT, hres(k, info.sbi)), (1, hT, hres(k, info.sbi)),
                    (2, vT, [ares("vT", jj, k) for jj in range(8)]),
                    (3, qT, [ares("qT", jj, k) for jj in range(8)])]
            for g, (sec, act, rr) in enumerate(srcs):
                def fn(g=g, sec=sec, act=act, rr=rr):
                    bi = balloc(1, bidx)
                    s.b[g] = bi

                    def f(e, d):
                        ins = None
                        for kk in range(8):
                            ins = e.matmul(bank(bi)[:, 0:N], lhsT=sl[:, sec, kk, :], rhs=act[:, kk, k.rel:k.rel + N],
                                           start=(kk == 0), stop=(kk == 7))
                        d(ins)
                    OP('pe', f, reads=[slr[sec]] + rr, writes=[pres[bi]])
                ops.append((base + 2.1 * sc * g, fn))

            def sig_a():
                s.ta = alloc('xa', [0, 1]) if tidx is None else alloc('p2a', tidx[0])
                claim(tres[s.ta], 2)
                OP('act', lambda e, d: d(e.activation(out=tap(s.ta)[:, 0:N], in_=bank(s.b[0])[:, 0:N], func=AF.Sigmoid)),
                   reads=[pres[s.b[0]]], writes=[tres[s.ta]])

            def sig_b():
                s.tb = alloc('c', [2, 3]) if tidx is None else alloc('p2b', tidx[1])
                claim(tres[s.tb], 2)
                OP('act', lambda e, d: d(e.activation(out=tap(s.tb)[:, 0:N], in_=bank(s.b[1])[:, 0:N], func=AF.Sigmoid)),
                   reads=[pres[s.b[1]]], writes=[tres[s.tb]])

            def mul_a():
                OP('dve', lambda e, d: d(e.tensor_tensor(out=tap(s.ta)[:, 0:N], in0=bank(s.b[2])[:, 0:N], in1=tap(s.ta)[:, 0:N], op=ALU.mult)),
                   reads=[pres[s.b[2]], tres[s.ta]], writes=[tres[s.ta]])

            def mul_b():
                OP('dve', lambda e, d: d(e.tensor_tensor(out=tap(s.tb)[:, 0:N], in0=bank(s.b[3])[:, 0:N], in1=tap(s.tb)[:, 0:N], op=ALU.mult)),
                   reads=[pres[s.b[3]], tres[s.tb]], writes=[tres[s.tb]])

            def add_m():
                OP('pool', lambda e, d: d(e.tensor_tensor(out=mT[:, j, k.rel:k.rel + N], in0=tap(s.ta)[:, 0:N], in1=tap(s.tb)[:, 0:N], op=ALU.add)),
                   reads=[tres[s.ta], tres[s.tb]], writes=[ares("mT", j, k)])
            ops += [(base + 2.3 * sc, sig_a), (base + 4.4 * sc, sig_b), (base + 6.5 * sc, mul_a), (base + 8.6 * sc, mul_b),
                    (base + 9.4 * sc, add_m)]
            return ops

        def phase3_ops(tl, base, pre_t=None):
            gi = tl.gi
            k = tl.blk
            s = Blk()
            ops = []

            def at(t):
                def deco(fn):
                    ops.append((base + t, fn))
                    return fn
                return deco

            def at_pre(t):
                def deco(fn):
                    ops.append(((base - 6.0) if pre_t is None else pre_t, fn))
                    return fn
                return deco

            @at_pre(0.0)
            def _():
                s.pi = alloc('pb', [0, 1])
                claim(res(f"pb{s.pi}"), 1)
                OP('pool', lambda e, d: d(e.dma_start(out=pb[s.pi][:], in_=tl.p)), writes=[res(f"pb{s.pi}")], dma=f"ldpb{s.pi}")

            @at(2.5)
            def _():
                s.bp = balloc()

                def f_trp(e, d):
                    ins = None
                    for kk in range(2):
                        ins = e.transpose(out=bank_bf(s.bp)[:, kk * 128:(kk + 1) * 128], in_=pb[s.pi][:, kk * 128:(kk + 1) * 128], identity=ident_b[:])
                    d(ins)
                OP('pe', f_trp, reads=[res(f"pb{s.pi}"), res("ident_b")], writes=[pres[s.bp]])

            @at(2.8)
            def _():
                s.pt = alloc('pTt', [0, 1, 2])
                claim(res(f"pTt{s.pt}"), 2)
                OP('act', lambda e, d: d(e.copy(out=pTt[s.pt][:], in_=bank_bf(s.bp)[:, 0:256].rearrange("p (k c) -> p k c", k=2))),
                   reads=[pres[s.bp]], writes=[res(f"pTt{s.pt}")])

            @at(3.0)
            def _():
                s.q = palloc()

                def f_o(e, d):
                    ins = None
                    for h in range(2):
                        for kk in range(8):
                            ins = e.matmul(ps[s.q][:, h * 512:(h + 1) * 512], lhsT=mT[:, kk, tl.rel:tl.rel + 128],
                                           rhs=wo_sb[:, kk, h * 512:(h + 1) * 512], start=(kk == 0), stop=(kk == 7))
                    d(ins)
                OP('pe', f_o, reads=[res("wo")] + [ares("mT", jj, k) for jj in range(8)], writes=[pres[2 * s.q], pres[2 * s.q + 1]])

            @at(4.0)
            def _():
                s.w = alloc('xt', [0, 1, 2])
                claim(wres(s.w)[0], 1)
                OP('sp', lambda e, d: d(e.dma_start(out=wap(s.w), in_=tl.x)), writes=wres(s.w), dma=f"ldW{s.w}")

            @at(6.8)
            def _():
                s.xb = alloc('x1b', [0, 1])
                claim(res(f"x1b{s.xb}"), 1)
                OP('act', lambda e, d: d(e.activation(out=x1b[s.xb][:], in_=ps[s.q][:, :], func=AF.Square, accum_out=stat(3, gi))),
                   reads=[pres[2 * s.q], pres[2 * s.q + 1], res("stats")], writes=[res(f"x1b{s.xb}"), res(f"ss2_{gi}")])

            @at(8.1)
            def _():
                OP('act', lambda e, d: d(e.activation(out=stat(4, gi), in_=stat(3, gi), func=AF.Identity, bias=epsc[:, 0:1], scale=1.0 / D)),
                   reads=[res(f"ss2_{gi}"), res("mhalf")], writes=[res(f"ms2_{gi}")])

            @at(8.6)
            def _():
                OP('pool', lambda e, d: d(e.tensor_tensor(out=stat(5, gi), in0=stat(4, gi), in1=mhalf[:, 0:1], op=ALU.pow)),
                   reads=[res(f"ms2_{gi}"), res("mhalf")], writes=[res(f"rstd2_{gi}")])

            @at(9.2)
            def _():
                s.ow = alloc('osb', [3, 4, 5])
                claim(wres(s.ow)[0], 5)
                OP('dve', lambda e, d: d(e.scalar_tensor_tensor(out=wap(s.ow), in0=ps[s.q][:, :], scalar=stat(5, gi), in1=gpost_bc[:],
                                                                 op0=ALU.mult, op1=ALU.mult)),
                   reads=[pres[2 * s.q], pres[2 * s.q + 1], res(f"rstd2_{gi}"), res("gpost")], writes=wres(s.ow))

            @at(11.0)
            def _():
                OP('dve', lambda e, d: d(e.tensor_tensor(out=wap(s.ow), in0=wap(s.ow), in1=wap(s.w), op=ALU.add)),
                   reads=wres(s.ow) + wres(s.w), writes=wres(s.ow))

            @at(12.4)
            def _():
                OP('act', lambda e, d: d(e.copy(out=x1b[s.xb][:], in_=wap(s.ow))), reads=wres(s.ow), writes=[res(f"x1b{s.xb}")])

            @at(13.8)
            def _():
                s.bt = balloc()

                def f_tr(e, d):
                    ins = None
                    for kk in range(8):
                        ins = e.transpose(out=bank_bf(s.bt)[:, kk * 128:(kk + 1) * 128], in_=x1b[s.xb][:, kk * 128:(kk + 1) * 128], identity=ident_b[:])
                    d(ins)
                OP('pe', f_tr, reads=[res(f"x1b{s.xb}"), res("ident_b")], writes=[pres[s.bt]])

            @at(14.6)
            def _():
                s.xi = alloc('x1T', [0])
                claim(res(f"x1T{s.xi}"), 2)
                OP('dve', lambda e, d: d(e.tensor_copy(out=x1T[s.xi][:], in_=bank_bf(s.bt).rearrange("p (k c) -> p k c", k=8))),
                   reads=[pres[s.bt]], writes=[res(f"x1T{s.xi}")])

            for h in range(2):
                hs = Blk()

                @at(15.5 + 2.2 * h)
                def _(h=h, hs=hs):
                    hs.bg = balloc(1, [5, 6, 7])

                    def f_g(e, d):
                        ins = None
                        for kk in range(8):
                            ins = e.matmul(bank(hs.bg), lhsT=x1T[s.xi][:, kk, :], rhs=wpg_sb[:, kk, h * 512:(h + 1) * 512],
                                           start=(kk == 0), stop=(kk == 7))
                        d(ins)
                    OP('pe', f_g, reads=[res(f"x1T{s.xi}"), res("wpg")], writes=[pres[hs.bg]])

                @at(15.6 + 2.6 * h)
                def _(h=h, hs=hs):
                    hs.be = balloc(1, [5, 6, 7])

                    def f_e(e, d):
                        ins = None
                        for kk in range(2):
                            ins = e.matmul(bank(hs.be), lhsT=pTt[s.pt][:, kk, :], rhs=wple_sb[:, kk, h * 512:(h + 1) * 512],
                                           start=(kk == 0), stop=(kk == 1))
                        d(ins)
                    OP('pe', f_e, reads=[res(f"pTt{s.pt}"), res("wple")], writes=[pres[hs.be]])

                @at(17.9 + 2.2 * h)
                def _(h=h, hs=hs):
                    hs.tg = alloc('sg', [12, 13])
                    claim(tres[hs.tg], 2)
                    OP('act', lambda e, d: d(e.activation(out=tap(hs.tg)[:, 0:512], in_=bank(hs.bg), func=AF.Sigmoid)),
                       reads=[pres[hs.bg]], writes=[tres[hs.tg]])

                @at(18.8 + 2.2 * h)
                def _(h=h, hs=hs):
                    OP('dve', lambda e, d: d(e.tensor_tensor(out=tap(hs.tg)[:, 0:512], in0=bank(hs.be), in1=tap(hs.tg)[:, 0:512], op=ALU.mult)),
                       reads=[pres[hs.be], tres[hs.tg]], writes=[tres[hs.tg]])

                @at(19.8 + 2.2 * h)
                def _(h=h, hs=hs):
                    OP('pool', lambda e, d: d(e.tensor_tensor(out=wap(s.ow)[:, h * 512:(h + 1) * 512], in0=tap(hs.tg)[:, 0:512],
                                                              in1=wap(s.ow)[:, h * 512:(h + 1) * 512], op=ALU.add)),
                       reads=[tres[hs.tg]] + wres(s.ow), writes=wres(s.ow))

            @at(23.6)
            def _():
                OP('sp', lambda e, d: d(e.dma_start(out=tl.y, in_=wap(s.ow))), reads=wres(s.ow), writes=[res(f"y_{tl.gi}")], dma=f"stW{s.ow}")
            return ops

        def final_state_outputs():
            fin = [
                (ncp, 2, lambda j: tail_u[:, j, 0:2], [f"tail_u{j}" for j in range(8)], 0, 0),
                (ncs, 8, lambda j: tail_u[:, j, 2:10], [f"tail_su{j}" for j in range(8)], 1, 1),
                (npp, 15, lambda j: tail_x[:, j, 0:15], [f"tail_x{j}" for j in range(8)], 2, 0),
                (nps, 60, lambda j: tail_x[:, j, 15:75], [f"tail_sx{j}" for j in range(8)], 3, 1),
            ]
            for dst, rows, src, rnames, q, w in fin:
                def f_t(e, d, rows=rows, src=src, q=q):
                    ins = None
                    for j in range(8):
                        ins = e.transpose(out=ps[q][0:rows, j * 128:(j + 1) * 128], in_=src(j), identity=ident_f[:])
                    d(ins)
                OP('pe', f_t, reads=[res(n) for n in rnames] + [res("ident_f")], writes=[pres[2 * q], pres[2 * q + 1]])
                ww = [3, 4, 5, 6][q]
                OP('act', lambda e, d, rows=rows, q=q, ww=ww: d(e.copy(out=wap(ww)[0:rows, :], in_=ps[q][0:rows, :])),
                   reads=[pres[2 * q], pres[2 * q + 1]], writes=wres(ww))
                OP('sp', lambda e, d, rows=rows, ww=ww, dst=dst: d(e.dma_start(out=dst[:, :], in_=wap(ww)[0:rows, :])),
                   reads=wres(ww), writes=[res(f"fin{q}")], dma=f"stW{ww}")


        nsb = len(SBS)
        items = []
        for n, tl in enumerate(sb_info[0].tiles):
            items += phase0_ops(tl, n * P0_PERIOD, xt_idx=(0, 1, 2, 3, 4, 5))
        items += [(0.3, lambda: issue_slab_load('head')), (5.0, lambda: issue_slab_load('rest')), (0.05, setup_gpre), (0.6, setup_stage_load), (3.0, setup_stage_transpose), (30.0, setup_gpost)]
        sl0, slr0 = slabs[0], slab_res[0]
        k00 = sb_info[0].blocks[0]
        t_first = 3 * P0_PERIOD + 7.2 + 0.3
        pe_items, pre_banks0 = phase1_pe_items(k00, sl0, slr0, t_first, 2.1, [0, 1, 2, 3, 4, 5])
        items += pe_items
        bank_state['singles'] = [6, 7]
        run_timed(items)
        for _ in range(N_SLAB - 1):
            issue_slab_load()
        load_wgrp()
        for sbi in range(nsb):
            info = sb_info[sbi]
            bank_state['singles'] = list(range(8))
            for j in range(8):
                sl, slr = slab_of(sbi, 1, j)
                for k in info.blocks:
                    phase1_unit(sbi, info, j, k, sl, slr, pre_banks=pre_banks0 if (sbi, j, k.idx) == (0, 0, 0) else None)
                    flush_b2(keep=(2 if j == 7 else 1) if j % 2 == 1 else 0)
                    if resident_loads and (j, k.idx) != (0, 0):
                        resident_loads.pop(0)()
                issue_slab_load()
            p3_items = []
            for i, tl in enumerate(info.tiles):
                p3_items += phase3_ops(tl, i * P3_PERIOD, pre_t=(-1000.0 + i) if i < 2 else None)
            for j in range(8):
                sl, slr = slab_of(sbi, 2, j)
                if j == 6:
                    run_timed([it for it in p3_items if it[0] < -500.0])
                    p3_items = [it for it in p3_items if it[0] >= -500.0]
                for k in info.blocks:
                    if j == 0 and k.idx == 0:
                        flush_b2(keep=len(info.blocks) - 1)
                    run_timed(phase2_ops(info, j, k, sl, slr, 0.0))
                    if j == 0 and k.idx == 0:
                        flush_b2(0)
                    if resident_loads:
                        resident_loads.pop(0)()
                issue_slab_load()
                if j == 2 and sbi == nsb - 1:
                    final_state_outputs()
            bank_state['singles'] = [4]
            rot['osb'] = 0
            rot['sg'] = 0
            nxt = sb_info[sbi + 1].tiles if sbi + 1 < nsb else []
            items = p3_items
            pair0 = rot.get('pair', 0) % 2
            for n, tl in enumerate(nxt):
                n_eff = n if n < len(info.tiles) else len(info.tiles) - 1.0
                items += phase0_ops(tl, n_eff * P3_PERIOD + 10.3, tr_bank=2 * ((pair0 + n) % 2))
            run_timed(items)

        P.finish('sp')
        P.emit_all(nc, st)
    return nc


_NC_CACHE = {}


def kernel(x_prompt, x_sample, p_prompt, p_sample, cache_conv, state_pool, g_pre, w_in, conv_w, conv_b,
           w_grp, pool_scale, w_a_out, w_b_out, w_o, g_post, w_ple, w_pgate):
    f = lambda a: np.ascontiguousarray(np.asarray(a, dtype=np.float32))
    x_prompt, x_sample, p_prompt, p_sample = f(x_prompt), f(x_sample), f(p_prompt), f(p_sample)
    cache_conv, state_pool = f(cache_conv), f(state_pool)
    shared = {
        "g_pre": f(g_pre).reshape(1, D), "w_in": np.ascontiguousarray(f(w_in).reshape(8, 128, 8, 8, 128).transpose(2, 3, 1, 0, 4)).reshape(64 * 128, D), "conv_w": f(conv_w).reshape(3, D),
        "conv_b": f(conv_b).reshape(1, D), "w_grp": f(w_grp).reshape(1024, 256), "pool_scale": f(pool_scale).reshape(1, D),
        "w_a_out": np.ascontiguousarray(f(w_a_out).reshape(8, 128, 8, 128).transpose(2, 1, 0, 3)).reshape(8 * 128, D), "w_b_out": np.ascontiguousarray(f(w_b_out).reshape(8, 128, 8, 128).transpose(2, 1, 0, 3)).reshape(8 * 128, D), "w_o": f(w_o).reshape(D, D),
        "g_post": f(g_post).reshape(1, D), "w_ple": f(w_ple).reshape(256, D), "w_pgate": f(w_pgate).reshape(D, D),
    }
    in_maps = []
    for c in range(N_CORES):
        m = dict(shared)
        m["xp"] = x_prompt[c]
        m["xs"] = x_sample[4 * c:4 * c + 4].reshape(128, D)
        m["pp"] = p_prompt[0, c]
        m["ps"] = p_sample[0, 4 * c:4 * c + 4].reshape(128, 256)
        m["cc"] = cache_conv[0, 4 * c:4 * c + 4].reshape(8, D)
        m["sp"] = state_pool[0, 4 * c:4 * c + 4].reshape(60, D)
        in_maps.append(m)
    if "nc" not in _NC_CACHE:
        _NC_CACHE["nc"] = build_program()
    nc = _NC_CACHE["nc"]
    res = run_bass_kernel_spmd(nc, in_maps, core_ids=list(range(N_CORES)))
    rs = res.results
    y_prompt = np.stack([rs[c]["yp"] for c in range(N_CORES)]).astype(np.float32)
    y_sample = np.concatenate([rs[c]["ys"].reshape(4, 32, D) for c in range(N_CORES)]).astype(np.float32)
    ncp = np.stack([rs[c]["ncp"] for c in range(N_CORES)])[None].astype(np.float32)
    ncs = np.concatenate([rs[c]["ncs"].reshape(4, 2, D) for c in range(N_CORES)])[None].astype(np.float32)
    npp = np.stack([rs[c]["npp"] for c in range(N_CORES)])[None].astype(np.float32)
    nps = np.concatenate([rs[c]["nps"].reshape(4, 15, D) for c in range(N_CORES)])[None].astype(np.float32)
    return (y_prompt, y_sample, ncp, ncs, npp, nps)
```

```python
from contextlib import ExitStack

import numpy as np
import concourse.bass as bass
import concourse.mybir as mybir
from concourse.bass_utils import run_bass_kernel_spmd

F32 = mybir.dt.float32
BF16 = mybir.dt.bfloat16
AF = mybir.ActivationFunctionType
ALU = mybir.AluOpType
ENGS = ('pe', 'act', 'dve', 'pool', 'sp')

N_CORES = 8
D = 1024
EPS = 1e-6
WINS = (2, 4, 8, 16)
S_XA, S_BA, S_CA, S_ZA, S_XB, S_ZB, S_GA, S_GB = range(8)

SBS = [
    [(0, 512), (512, 512)],
    [(1024, 512), (1536, 512), None],
]
MAXP = 1024
NT = MAXP + 128
SOFF = 16 + MAXP
UW = SOFF + 4 * 48
N_SLAB = 2
NW = 7
TBW = 528
P0_PERIOD = 2.0
P3_PERIOD = 9.5


class Res:
    __slots__ = ('name', 'lw', 'rs', 'pending', 'epoch')

    def __init__(self, name):
        self.name = name
        self.lw = None
        self.rs = {}
        self.pending = 0
        self.epoch = None


class Prog:
    def __init__(self):
        self.ops = {e: [] for e in ENGS}
        self.cnt = {}
        self.known = {e: {} for e in ENGS}

    def op(self, eng, emit, reads=(), writes=(), dma=None, n_dma=1):
        if dma is None:
            clk, inc = eng, 1
        else:
            clk, inc = dma, 16 * n_dma
        deps = {}

        def add(d, raw):
            if d is None:
                return
            c, v = d
            if deps.get(c, 0) < v:
                deps[c] = v
        for r in reads:
            add(r.lw, True)
            if r.pending > 0:
                r.pending -= 1
        for w in writes:
            add(w.lw, False)
            for c, v in w.rs.items():
                add((c, v), False)
        waits = []
        kn = self.known[eng]
        for c, v in deps.items():
            if kn.get(c, 0) >= v:
                continue
            kn[c] = v
            waits.append((c, v))
        self.cnt[clk] = self.cnt.get(clk, 0) + inc
        seq = self.cnt[clk]
        for w in writes:
            w.lw = (clk, seq)
            w.rs = {}
        for r in reads:
            if r.rs.get(clk, 0) < seq:
                r.rs[clk] = seq
        self.ops[eng].append((waits, emit, clk, 16 if dma is not None else 1))

    def finish(self, eng='sp'):
        waits = [(c, v) for c, v in self.cnt.items() if c != eng]
        self.ops[eng].append((waits, None, None, 0))

    def emit_all(self, nc, stack):
        sems = {}
        for c in self.cnt:
            sems[c] = stack.enter_context(nc.semaphore("s_" + c))
        block = stack.enter_context(nc.Block())
        names = {'pe': 'tensor', 'act': 'scalar', 'dve': 'vector', 'pool': 'gpsimd', 'sp': 'sync'}
        for eng in ENGS:
            ops = self.ops[eng]

            def body(e, ops=ops):
                for waits, emit, clk, inc in ops:
                    for c, v in waits:
                        e.wait_ge(sems[c], v)
                    if emit is None:
                        continue
                    sem = sems[clk]
                    emit(e, lambda ins, sem=sem, inc=inc: ins.then_inc(sem, inc))
            getattr(block, names[eng])(body)


class Blk:
    pass


def build_program():
    nc = bass.Bass("TRN2", target_bir_lowering=False)

    def din(name, shape):
        return nc.dram_tensor(name, shape, F32, kind="ExternalInput").ap()

    def dout(name, shape):
        return nc.dram_tensor(name, shape, F32, kind="ExternalOutput").ap()

    xp = din("xp", [2048, D])
    xs = din("xs", [128, D])
    pp = din("pp", [2048, 256])
    psm = din("ps", [128, 256])
    ccd = din("cc", [8, D])
    spd = din("sp", [60, D])
    g_pre = din("g_pre", [1, D])
    w_in = din("w_in", [64 * 128, D])
    conv_w = din("conv_w", [3, D])
    conv_b = din("conv_b", [1, D])
    w_grp = din("w_grp", [1024, 256])
    pool_scale = din("pool_scale", [1, D])
    w_a_out = din("w_a_out", [8 * 128, D])
    w_b_out = din("w_b_out", [8 * 128, D])
    w_o = din("w_o", [D, D])
    g_post = din("g_post", [1, D])
    w_ple = din("w_ple", [256, D])
    w_pgate = din("w_pgate", [D, D])
    yp = dout("yp", [2048, D])
    ys = dout("ys", [128, D])
    ncp = dout("ncp", [2, D])
    ncs = dout("ncs", [8, D])
    npp = dout("npp", [15, D])
    nps = dout("nps", [60, D])

    P = Prog()
    OP = P.op
    st = ExitStack()
    with st:
        def sb(name, shape, dt):
            return st.enter_context(nc.sbuf_tensor(name, shape, dt))

        ident_f = sb("ident_f", [128, 128], F32)
        ident_b = sb("ident_b", [128, 128], BF16)
        smallT = sb("smallT", [128, 8, 73], F32)
        gpre_bc = sb("gpre_bc", [128, D], F32)
        gpost_bc = sb("gpost_bc", [128, D], F32)
        invcnt = sb("invcnt", [128, 4, 16], F32)
        mhalf = sb("mhalf", [128, 1], F32)
        epsc = sb("epsc", [128, 1], F32)
        stats = sb("stats", [128, 6, 18], F32)
        hb = [sb(f"hb{i}", [128, D], BF16) for i in range(3)]
        x1b = [sb(f"x1b{i}", [128, D], BF16) for i in range(2)]
        pb = [sb(f"pb{i}", [128, 256], BF16) for i in range(2)]
        pTt = [sb(f"pTt{i}", [128, 2, 128], BF16) for i in range(3)]
        x1T = [sb(f"x1T{i}", [128, 8, 128], BF16) for i in range(1)]
        hT = sb("hT", [128, 8, NT], BF16)
        vT = sb("vT", [128, 8, NT], BF16)
        qT = sb("qT", [128, 8, NT], BF16)
        mT = sb("mT", [128, 8, NT], BF16)
        U = sb("U", [128, UW], F32)
        XB = sb("XB", [128, UW], F32)
        tail_u = sb("tail_u", [128, 8, 10], F32)
        tail_x = sb("tail_x", [128, 8, 75], F32)
        slabs = [sb(f"slab{i}", [128, 6, 8, 128], BF16) for i in range(N_SLAB)]
        wo_sb = sb("wo_sb", [128, 8, D], BF16)
        wpg_sb = sb("wpg_sb", [128, 8, D], BF16)
        wple_sb = sb("wple_sb", [128, 2, D], BF16)
        wgrp_sb = sb("wgrp_sb", [128, 4, 2, 256], BF16)
        arena = sb("arena", [128, NW * 2 * TBW], F32)
        ps = [st.enter_context(nc.psum_tensor(f"ps{q}", [128, 1024], F32)) for q in range(4)]

        R = {}

        def res(name):
            if name not in R:
                R[name] = Res(name)
            return R[name]

        pres = [res(f"bank{b}") for b in range(8)]
        tres = [res(f"T{i}") for i in range(2 * NW)]
        slab_res = [[res(f"slab{i}_{si}") for si in range(6)] for i in range(N_SLAB)]

        def bank(b):
            return ps[b // 2][:, (b % 2) * 512:(b % 2) * 512 + 512]

        def bank_bf(b):
            return ps[b // 2][:].bitcast(BF16)[:, (b % 2) * 1024:(b % 2) * 1024 + 1024]

        def tap(i):
            return arena[:, i * TBW:(i + 1) * TBW]

        def wap(w):
            return arena[:, w * 2 * TBW:w * 2 * TBW + 1024]

        def wres(w):
            return [tres[2 * w], tres[2 * w + 1]]

        rot = {}

        def alloc(role, idxs):
            i = rot.get(role, 0)
            rot[role] = i + 1
            return idxs[i % len(idxs)]

        def claim(r, nreads):
            assert r.pending == 0, f"liveness violation on {r.name}: {r.pending} readers not yet emitted"
            r.pending = nreads

        bank_state = {'singles': list(range(8))}

        def balloc(nreads=1, idxs=None):
            idxs = bank_state['singles'] if idxs is None else idxs
            b = alloc('bank:' + ','.join(map(str, idxs)), idxs)
            claim(pres[b], nreads)
            return b

        def palloc(nreads=2):
            q = alloc('pair', [0, 1])
            claim(pres[2 * q], nreads)
            claim(pres[2 * q + 1], nreads)
            return q

        stage_w = 6
        stage = wap(stage_w)
        OP('pool', lambda e, d: d(e.memset(tap(11)[:, 0:128], 1.0)), writes=[tres[11]])
        OP('pool', lambda e, d: d(e.affine_select(out=ident_f[:], in_=tap(11)[:, 0:128], pattern=[[-1, 128]],
                                                   compare_op=ALU.is_equal, fill=0.0, base=0, channel_multiplier=1)),
           reads=[tres[11]], writes=[res("ident_f")])
        def f_consts(e, d):
            e.memset(epsc[:], EPS)
            d(e.memset(mhalf[:], -0.5))
        OP('pool', f_consts, writes=[res("mhalf")])

        def f_invcnt(e, d):
            ins = None
            for g, W in enumerate(WINS):
                ins = e.memset(invcnt[:, g, W - 1:16], 1.0 / W)
                for t in range(W - 1):
                    ins = e.memset(invcnt[:, g, t:t + 1], 1.0 / (t + 1))
            d(ins)
        OP('pool', f_invcnt, writes=[res("invcnt")])

        def f_zero(e, d):
            e.memset(U[:, 0:16], 0.0)
            e.memset(XB[:, 0:16], 0.0)
            d(e.memset(stats[:], 0.0))
        OP('pool', f_zero, writes=[res("U0"), res("XB0"), res("stats")])

        OP('dve', lambda e, d: d(e.tensor_copy(out=ident_b[:], in_=ident_f[:])), reads=[res("ident_f")], writes=[res("ident_b")])

        def setup_stage_load():
            def f_stage(e, d):
                d(e.dma_start(out=stage[0:3, :], in_=conv_w[:, :]))
                d(e.dma_start(out=stage[3:4, :], in_=conv_b[:, :]))
                d(e.dma_start(out=stage[4:5, :], in_=pool_scale[:, :]))
                d(e.dma_start(out=stage[5:13, :], in_=ccd[:, :]))
                d(e.dma_start(out=stage[13:73, :], in_=spd[:, :]))
            OP('sp', f_stage, writes=wres(stage_w), dma=f"ldW{stage_w}", n_dma=5)

        def setup_gpre():
            OP('sp', lambda e, d: d(e.dma_start(out=gpre_bc[:], in_=g_pre[0:1, :].partition_broadcast(128))), writes=[res("gpre")], dma="ld_gpre")

        def setup_gpost():
            OP('sp', lambda e, d: d(e.dma_start(out=gpost_bc[:], in_=g_post[0:1, :].partition_broadcast(128))), writes=[res("gpost")], dma="ld_gpost")

        def setup_stage_transpose():
            def f_tr_stage(e, d):
                ins = None
                for j in range(8):
                    q = ps[j // 4]
                    ins = e.transpose(out=q[:, (j % 4) * 73:(j % 4) * 73 + 73], in_=stage[0:73, j * 128:(j + 1) * 128],
                                      identity=ident_f[0:73, 0:73])
                d(ins)
            claim(pres[0], 1)
            claim(pres[2], 1)
            OP('pe', f_tr_stage, reads=wres(stage_w) + [res("ident_f")], writes=[pres[0], pres[2]])

            def f_cp_stage(e, d):
                e.tensor_copy(out=smallT[:, 0:4, :], in_=ps[0][:, 0:292].rearrange("p (j c) -> p j c", j=4))
                d(e.tensor_copy(out=smallT[:, 4:8, :], in_=ps[1][:, 0:292].rearrange("p (j c) -> p j c", j=4)))
            OP('dve', f_cp_stage, reads=[pres[0], pres[2]], writes=[res("smallT")])

        OP('act', lambda e, d: d(e.activation(out=stats[:, 5, 17:18], in_=mhalf[:, 0:1], func=AF.Square)),
           reads=[res("mhalf"), res("stats")], writes=[res("rstd2_dummy")])

        slab_seq = []
        for sbi in range(len(SBS)):
            for ph in (1, 2):
                for j in range(8):
                    slab_seq.append((sbi, ph, j))
        slab_next = [0]

        def issue_slab_load(part=None):
            idx = slab_next[0]
            if idx >= len(slab_seq):
                return
            if part != 'head':
                slab_next[0] += 1
            sbi, ph, j = slab_seq[idx]
            slot = idx % N_SLAB
            sl = slabs[slot]
            def blk(t, b):
                return t[b * 128:(b + 1) * 128, :].rearrange("p (k c) -> p k c", k=8)
            if ph == 1:
                srcs = [blk(w_in, sec * 8 + j) for sec in (S_XA, S_BA, S_CA, S_ZA, S_XB, S_ZB)]
            else:
                srcs = [blk(w_in, S_GA * 8 + j), blk(w_in, S_GB * 8 + j), blk(w_a_out, j), blk(w_b_out, j)]

            order = (0, 2, 1, 3, 4, 5) if ph == 1 else (0, 1, 2, 3)
            if part == 'head':
                order = order[:1]
            elif part == 'rest':
                order = order[1:]
            for si in order:
                src = srcs[si]
                OP('pool', lambda e, d, sl=sl, si=si, src=src: d(e.dma_start(out=sl[:, si, :, :], in_=src)),
                   writes=[slab_res[slot][si]], dma=f"slab{slot}_{si}")

        def slab_of(sbi, ph, j):
            idx = slab_seq.index((sbi, ph, j))
            assert idx < slab_next[0], "slab not issued yet"
            return slabs[idx % N_SLAB], slab_res[idx % N_SLAB]

        def load_wgrp():
            OP('pool', lambda e, d: d(e.dma_start(out=wgrp_sb[:], in_=w_grp.rearrange("(g k p) c -> p g k c", g=4, k=2, p=128))),
               writes=[res("wgrp")], dma="ld_wgrp")

        resident_loads = []
        for name, dst, src, nk in (("wo", wo_sb, w_o, 8), ("wpg", wpg_sb, w_pgate, 8), ("wple", wple_sb, w_ple, 2)):
            for kk in range(nk):
                def ld(name=name, dst=dst, src=src, kk=kk):
                    OP('pool', lambda e, d: d(e.dma_start(out=dst[:, kk, :], in_=src[kk * 128:(kk + 1) * 128, :])),
                       writes=[res(name)], dma="ld_" + name)
                resident_loads.append(ld)


        sb_info = []
        for sbi, blks in enumerate(SBS):
            info = Blk()
            info.sbi = sbi
            info.blocks = []
            rel = 0
            info.p0 = None
            for bi, b in enumerate(blks):
                k = Blk()
                k.idx = bi
                if b is not None:
                    k.samp = False
                    k.t0g, k.N = b
                    if info.p0 is None:
                        info.p0 = k.t0g
                    k.rel = k.t0g - info.p0
                else:
                    k.samp = True
                    k.N = 128
                    k.rel = rel
                rel = k.rel + k.N
                info.blocks.append(k)
            info.Lp = sum(k.N for k in info.blocks if not k.samp)
            info.has_samp = any(k.samp for k in info.blocks)
            info.tiles = []
            for k in info.blocks:
                for t in range(k.N // 128):
                    tl = Blk()
                    tl.sbi = sbi
                    tl.rel = k.rel + t * 128
                    tl.blk = k
                    if k.samp:
                        tl.gi = 16
                        tl.x = xs[:, :]
                        tl.p = psm[:, :]
                        tl.y = ys[:, :]
                    else:
                        r0 = k.t0g + t * 128
                        tl.gi = r0 // 128
                        tl.x = xp[r0:r0 + 128, :]
                        tl.p = pp[r0:r0 + 128, :]
                        tl.y = yp[r0:r0 + 128, :]
                    info.tiles.append(tl)
            sb_info.append(info)

        def ares(name, j, k):
            return res(f"{name}_{j}_{k.idx}")

        def hres(k, sbi=None):
            rr = [res(f"hT_t{t}") for t in range(k.rel // 128, (k.rel + k.N) // 128)]
            if sbi is not None:
                for r in rr:
                    assert r.epoch == sbi, f"{r.name} read for SB{sbi} before its head wrote it (epoch {r.epoch})"
            return rr

        def hres_tile(tl):
            return res(f"hT_t{tl.rel // 128}")

        def stat(row, gi):
            return stats[:, row, gi:gi + 1]

        def run_timed(items):
            items = sorted(enumerate(items), key=lambda t: (t[1][0], t[0]))
            for _, (_, fn) in items:
                fn()

        def phase0_ops(tl, base, tr_bank=None, xt_idx=(0, 1, 2), load_lead=0.0):
            gi = tl.gi
            s = Blk()
            ssr = res(f"ss_{gi}")

            def a():
                s.w = alloc('xt' if len(xt_idx) == 3 else 'xt_head', list(xt_idx))
                claim(wres(s.w)[0], 2)
                OP('sp', lambda e, d: d(e.dma_start(out=wap(s.w), in_=tl.x)), writes=wres(s.w), dma=f"ldW{s.w}")

            def b():
                s.hi = alloc('hb', [0, 1, 2])
                claim(res(f"hb{s.hi}"), 1)
                OP('act', lambda e, d: d(e.activation(out=hb[s.hi][:], in_=wap(s.w), func=AF.Square, accum_out=stat(0, gi))),
                   reads=wres(s.w) + [res("stats")], writes=[res(f"hb{s.hi}"), ssr])

            def c():
                OP('act', lambda e, d: d(e.activation(out=stat(1, gi), in_=stat(0, gi), func=AF.Identity, bias=epsc[:, 0:1], scale=1.0 / D)),
                   reads=[ssr, res("mhalf")], writes=[res(f"ms_{gi}")])

            def c2():
                OP('pool', lambda e, d: d(e.tensor_tensor(out=stat(2, gi), in0=stat(1, gi), in1=mhalf[:, 0:1], op=ALU.pow)),
                   reads=[res(f"ms_{gi}"), res("mhalf")], writes=[res(f"rstd_{gi}")])

            def dd():
                OP('dve', lambda e, d: d(e.scalar_tensor_tensor(out=hb[s.hi][:], in0=wap(s.w), scalar=stat(2, gi), in1=gpre_bc[:],
                                                                 op0=ALU.mult, op1=ALU.mult)),
                   reads=wres(s.w) + [res(f"rstd_{gi}"), res("gpre")], writes=[res(f"hb{s.hi}")])

            def ee():
                s.b = balloc() if tr_bank is None else balloc(1, [tr_bank])

                def f_tr(e, d):
                    ins = None
                    for k in range(8):
                        ins = e.transpose(out=bank_bf(s.b)[:, k * 128:(k + 1) * 128], in_=hb[s.hi][:, k * 128:(k + 1) * 128], identity=ident_b[:])
                    d(ins)
                OP('pe', f_tr, reads=[res(f"hb{s.hi}"), res("ident_b")], writes=[pres[s.b]])

            def ff():
                if gi % 2 == 0:
                    OP('act', lambda e, d: d(e.copy(out=hT[:, :, tl.rel:tl.rel + 128],
                                                    in_=bank_bf(s.b).rearrange("p (k c) -> p k c", k=8))),
                       reads=[pres[s.b]], writes=[hres_tile(tl)])
                    hres_tile(tl).epoch = tl.sbi
                else:
                    OP('dve', lambda e, d: d(e.tensor_copy(out=hT[:, :, tl.rel:tl.rel + 128],
                                                           in_=bank_bf(s.b).rearrange("p (k c) -> p k c", k=8))),
                       reads=[pres[s.b]], writes=[hres_tile(tl)])
                    hres_tile(tl).epoch = tl.sbi
            return [(base - load_lead, a), (base + 2.3, b), (base + 3.6, c), (base + 4.0, c2), (base + 4.8, dd), (base + 6.4, ee), (base + 7.2, ff)]

        def blkview(ap, k):
            if k.samp:
                return ap[:, 0:128].rearrange("p (s c) -> p s c", s=4)
            return ap[:, 0:k.N]

        def actview(t, j, k):
            if k.samp:
                return t[:, j, k.rel:k.rel + 128].rearrange("p (s c) -> p s c", s=4)
            return t[:, j, k.rel:k.rel + k.N]

        def uwin(buf, k, lo, hi):
            if k.samp:
                return buf[:, SOFF:SOFF + 192].rearrange("p (s c) -> p s c", s=4)[:, :, lo:hi]
            return buf[:, k.rel + lo:k.rel + hi]

        def twin(i, k, lo, hi):
            if k.samp:
                return tap(i)[:, 0:192].rearrange("p (s c) -> p s c", s=4)[:, :, lo:hi]
            return tap(i)[:, lo:hi]

        def ures(name, k):
            return res(f"{name}{k.idx}")

        pending_b2 = []

        def phase1_pe_items(k, sl, slr, t0, dt, bidx):
            N = k.N
            banks = []
            items = []
            for g, sec in enumerate((0, 2, 1, 3, 4, 5)):
                def fn(sec=sec):
                    bi = balloc(1, bidx)
                    banks.append(bi)

                    def f(e, d):
                        ins = None
                        for kk in range(8):
                            ins = e.matmul(bank(bi)[:, 0:N], lhsT=sl[:, sec, kk, :], rhs=hT[:, kk, k.rel:k.rel + N],
                                           start=(kk == 0), stop=(kk == 7))
                        d(ins)
                    OP('pe', f, reads=[slr[sec]] + hres(k, 0), writes=[pres[bi]])
                items.append((t0 + dt * g, fn))
            return items, banks

        def phase1_unit(sbi, info, j, k, sl, slr, pre_banks=None):
            N = k.N
            E = 48 if k.samp else 16 + N
            jp, par = j // 2, j % 2
            W = WINS[jp]
            first_prompt = (not k.samp) and k.rel == 0
            last_prompt = (not k.samp) and (k.rel + N == info.Lp)
            if pre_banks is not None:
                banks = pre_banks
            else:
                banks = [balloc() for _ in range(6)]
                for bi, sec in zip(banks, (0, 2, 1, 3, 4, 5)):
                    def f(e, d, bi=bi, sec=sec):
                        ins = None
                        for kk in range(8):
                            ins = e.matmul(bank(bi)[:, 0:N], lhsT=sl[:, sec, kk, :], rhs=hT[:, kk, k.rel:k.rel + N],
                                           start=(kk == 0), stop=(kk == 7))
                        d(ins)
                    OP('pe', f, reads=[slr[sec]] + hres(k, sbi), writes=[pres[bi]])
            b_xa, b_ca, b_ba, b_za, b_xb, b_zb = banks
            cw = lambda t: smallT[:, j, t:t + 1]
            cbias = smallT[:, j, 3:4]
            psc = smallT[:, j, 4:5]
            sm = res("smallT")
            ur = ures("U", k)
            xr = ures("XB", k)
            prev_u = [ures("U", info.blocks[k.idx - 1])] if (not k.samp and k.idx > 0) else []
            prev_x = [ures("XB", info.blocks[k.idx - 1])] if (not k.samp and k.idx > 0) else []
            if first_prompt and sbi > 0:
                OP('dve', lambda e, d: d(e.tensor_copy(out=U[:, 14:16], in_=tail_u[:, j, 0:2])),
                   reads=[res(f"tail_u{j}")], writes=[ur])
                OP('pool', lambda e, d: d(e.tensor_copy(out=XB[:, 1:16], in_=tail_x[:, j, 0:15])),
                   reads=[res(f"tail_x{j}")], writes=[xr])
            if k.samp:
                OP('dve', lambda e, d: d(e.tensor_copy(out=uwin(U, k, 14, 16),
                                                        in_=smallT[:, j, 5:13].rearrange("p (s c) -> p s c", s=4))),
                   reads=[sm], writes=[ur])
                OP('pool', lambda e, d: d(e.tensor_copy(out=uwin(XB, k, 1, 16),
                                                         in_=smallT[:, j, 13:73].rearrange("p (s c) -> p s c", s=4))),
                   reads=[sm], writes=[xr])
            t_xa = alloc('xa', [0, 1])
            OP('act', lambda e, d: d(e.copy(out=blkview(tap(t_xa), k), in_=blkview(bank(b_xa), k))),
               reads=[pres[b_xa]], writes=[tres[t_xa]])
            OP('dve', lambda e, d: d(e.tensor_tensor(out=uwin(U, k, 16, E), in0=blkview(bank(b_ca), k),
                                                      in1=blkview(tap(t_xa), k), op=ALU.mult)),
               reads=[pres[b_ca], tres[t_xa]], writes=[ur])
            t_c = alloc('c', [2, 3])
            cv = blkview(tap(t_c), k)
            OP('act', lambda e, d: d(e.activation(out=cv, in_=uwin(U, k, 16, E), func=AF.Identity, bias=cbias, scale=cw(2))),
               reads=[ur, sm], writes=[tres[t_c]])
            OP('dve', lambda e, d: d(e.scalar_tensor_tensor(out=cv, in0=uwin(U, k, 15, E - 1), scalar=cw(1), in1=cv,
                                                             op0=ALU.mult, op1=ALU.add)),
               reads=[ur, sm, tres[t_c]] + prev_u, writes=[tres[t_c]])
            OP('dve', lambda e, d: d(e.scalar_tensor_tensor(out=cv, in0=uwin(U, k, 14, E - 2), scalar=cw(0), in1=cv,
                                                             op0=ALU.mult, op1=ALU.add)),
               reads=[ur, sm, tres[t_c]] + prev_u, writes=[tres[t_c]])
            t_sz = alloc('sz', [4, 5])
            OP('act', lambda e, d: d(e.activation(out=blkview(tap(t_sz), k), in_=blkview(bank(b_za), k), func=AF.Silu)),
               reads=[pres[b_za]], writes=[tres[t_sz]])
            OP('dve', lambda e, d: d(e.tensor_tensor(out=cv, in0=blkview(bank(b_ba), k), in1=cv, op=ALU.mult)),
               reads=[pres[b_ba], tres[t_c]], writes=[tres[t_c]])
            OP('dve', lambda e, d: d(e.tensor_tensor(out=actview(vT, j, k), in0=cv, in1=blkview(tap(t_sz), k), op=ALU.mult)),
               reads=[tres[t_c], tres[t_sz]], writes=[ares("vT", j, k)])
            OP('act', lambda e, d: d(e.copy(out=uwin(XB, k, 16, E), in_=blkview(bank(b_xb), k))),
               reads=[pres[b_xb]], writes=[xr])
            OP('act', lambda e, d: d(e.activation(out=actview(mT, j, k), in_=blkview(bank(b_zb), k), func=AF.Silu)),
               reads=[pres[b_zb]], writes=[ares("mT", j, k)])
            cur = None
            cur_res = [xr] + prev_x
            lo = 1
            w = 1
            tsel = 0
            t_s = [alloc('sA', [6, 7]), alloc('sB', [8, 9])]
            while w < W:
                nlo = lo + w
                out_i = t_s[tsel]

                def f(e, d, cur=cur, lo=lo, nlo=nlo, w=w, out_i=out_i):
                    if cur is None:
                        a = uwin(XB, k, nlo, E)
                        b_ = uwin(XB, k, nlo - w, E - w)
                    else:
                        a = twin(cur, k, nlo, E)
                        b_ = twin(cur, k, nlo - w, E - w)
                    d(e.tensor_tensor(out=twin(out_i, k, nlo, E), in0=a, in1=b_, op=ALU.add))
                OP('pool', f, reads=cur_res, writes=[tres[out_i]])
                cur = out_i
                cur_res = [tres[out_i]]
                lo = nlo
                w *= 2
                tsel ^= 1
            t_sw = cur
            OP('dve', lambda e, d: d(e.scalar_tensor_tensor(out=actview(qT, j, k), in0=twin(t_sw, k, 16, E), scalar=1.0 / W,
                                                             in1=uwin(XB, k, 16, E), op0=ALU.mult, op1=ALU.subtract)),
               reads=[tres[t_sw], xr], writes=[ares("qT", j, k)])
            if first_prompt and sbi == 0:
                t_fx = 10
                OP('dve', lambda e, d: d(e.tensor_tensor(out=tap(t_fx)[:, 0:16], in0=tap(t_sw)[:, 16:32], in1=invcnt[:, jp, :], op=ALU.mult)),
                   reads=[tres[t_sw], res("invcnt")], writes=[tres[t_fx]])
                OP('dve', lambda e, d: d(e.tensor_tensor(out=qT[:, j, 0:16], in0=tap(t_fx)[:, 0:16], in1=XB[:, 16:32], op=ALU.subtract)),
                   reads=[tres[t_fx], xr], writes=[ares("qT", j, k)])
            if last_prompt:
                Lp = info.Lp
                OP('dve', lambda e, d: d(e.tensor_copy(out=tail_u[:, j, 0:2], in_=U[:, 16 + Lp - 2:16 + Lp])),
                   reads=[ur], writes=[res(f"tail_u{j}")])
                OP('pool', lambda e, d: d(e.tensor_copy(out=tail_x[:, j, 0:15], in_=XB[:, 16 + Lp - 15:16 + Lp])),
                   reads=[xr], writes=[res(f"tail_x{j}")])
            if k.samp:
                OP('dve', lambda e, d: d(e.tensor_copy(out=tail_u[:, j, 2:10].rearrange("p (s c) -> p s c", s=4),
                                                        in_=uwin(U, k, 46, 48))),
                   reads=[ur], writes=[res(f"tail_su{j}")])
                OP('pool', lambda e, d: d(e.tensor_copy(out=tail_x[:, j, 15:75].rearrange("p (s c) -> p s c", s=4),
                                                         in_=uwin(XB, k, 33, 48))),
                   reads=[xr], writes=[res(f"tail_sx{j}")])
            if par == 1:
                def b2(jp=jp, k=k):
                    N_ = k.N
                    for c in range(2):
                        bb = balloc()
                        jj = 2 * jp + c

                        def f(e, d, bb=bb, c=c):
                            ins = None
                            for kk in range(2):
                                ins = e.matmul(bank(bb)[:, 0:N_], lhsT=wgrp_sb[:, jp, kk, c * 128:(c + 1) * 128],
                                               rhs=qT[:, 2 * jp + kk, k.rel:k.rel + N_], start=(kk == 0), stop=(kk == 1))
                            d(ins)
                        OP('pe', f, reads=[res("wgrp"), ares("qT", 2 * jp, k), ares("qT", 2 * jp + 1, k)], writes=[pres[bb]])
                        pending_q.append((bb, jj, k))
                pending_b2.append(b2)

        pending_q = []

        def flush_q():
            while pending_q:
                bb, jj, k = pending_q.pop(0)
                OP('dve', lambda e, d, bb=bb, jj=jj, k=k: d(e.scalar_tensor_tensor(
                    out=actview(qT, jj, k), in0=blkview(bank(bb), k), scalar=smallT[:, jj, 4:5],
                    in1=actview(mT, jj, k), op0=ALU.mult, op1=ALU.mult)),
                   reads=[pres[bb], res("smallT"), ares("mT", jj, k)], writes=[ares("qT", jj, k)])

        def flush_b2(keep=0):
            while len(pending_b2) > keep:
                pending_b2.pop(0)()
                flush_q()

        def phase2_ops(info, j, k, sl, slr, base, bidx=None, tidx=None):
            N = k.N
            sc = N / 512.0
            s = Blk()
            s.b = {}
            ops = []
            srcs = [(0, hT, hres(k, info.sbi)), (1, hT, hres(k, info.sbi)),
                    (2, vT, [ares("vT", jj, k) for jj in range(8)]),
                    (3, qT, [ares("qT", jj, k) for jj in range(8)])]
            for g, (sec, act, rr) in enumerate(srcs):
                def fn(g=g, sec=sec, act=act, rr=rr):
                    bi = balloc(1, bidx)
                    s.b[g] = bi

                    def f(e, d):
                        ins = None
                        for kk in range(8):
                            ins = e.matmul(bank(bi)[:, 0:N], lhsT=sl[:, sec, kk, :], rhs=act[:, kk, k.rel:k.rel + N],
                                           start=(kk == 0), stop=(kk == 7))
                        d(ins)
                    OP('pe', f, reads=[slr[sec]] + rr, writes=[pres[bi]])
                ops.append((base + 2.1 * sc * g, fn))

            def sig_a():
                s.ta = alloc('xa', [0, 1]) if tidx is None else alloc('p2a', tidx[0])
                claim(tres[s.ta], 2)
                OP('act', lambda e, d: d(e.activation(out=tap(s.ta)[:, 0:N], in_=bank(s.b[0])[:, 0:N], func=AF.Sigmoid)),
                   reads=[pres[s.b[0]]], writes=[tres[s.ta]])

            def sig_b():
                s.tb = alloc('c', [2, 3]) if tidx is None else alloc('p2b', tidx[1])
                claim(tres[s.tb], 2)
                OP('act', lambda e, d: d(e.activation(out=tap(s.tb)[:, 0:N], in_=bank(s.b[1])[:, 0:N], func=AF.Sigmoid)),
                   reads=[pres[s.b[1]]], writes=[tres[s.tb]])

            def mul_a():
                OP('dve', lambda e, d: d(e.tensor_tensor(out=tap(s.ta)[:, 0:N], in0=bank(s.b[2])[:, 0:N], in1=tap(s.ta)[:, 0:N], op=ALU.mult)),
                   reads=[pres[s.b[2]], tres[s.ta]], writes=[tres[s.ta]])

            def mul_b():
                OP('dve', lambda e, d: d(e.tensor_tensor(out=tap(s.tb)[:, 0:N], in0=bank(s.b[3])[:, 0:N], in1=tap(s.tb)[:, 0:N], op=ALU.mult)),
                   reads=[pres[s.b[3]], tres[s.tb]], writes=[tres[s.tb]])

            def add_m():
                OP('pool', lambda e, d: d(e.tensor_tensor(out=mT[:, j, k.rel:k.rel + N], in0=tap(s.ta)[:, 0:N], in1=tap(s.tb)[:, 0:N], op=ALU.add)),
                   reads=[tres[s.ta], tres[s.tb]], writes=[ares("mT", j, k)])
            ops += [(base + 2.3 * sc, sig_a), (base + 4.4 * sc, sig_b), (base + 6.5 * sc, mul_a), (base + 8.6 * sc, mul_b),
                    (base + 9.4 * sc, add_m)]
            return ops

        def phase3_ops(tl, base, pre_t=None):
            gi = tl.gi
            k = tl.blk
            s = Blk()
            ops = []

            def at(t):
                def deco(fn):
                    ops.append((base + t, fn))
                    return fn
                return deco

            def at_pre(t):
                def deco(fn):
                    ops.append(((base - 6.0) if pre_t is None else pre_t, fn))
                    return fn
                return deco

            @at_pre(0.0)
            def _():
                s.pi = alloc('pb', [0, 1])
                claim(res(f"pb{s.pi}"), 1)
                OP('pool', lambda e, d: d(e.dma_start(out=pb[s.pi][:], in_=tl.p)), writes=[res(f"pb{s.pi}")], dma=f"ldpb{s.pi}")

            @at(2.5)
            def _():
                s.bp = balloc()

                def f_trp(e, d):
                    ins = None
                    for kk in range(2):
                        ins = e.transpose(out=bank_bf(s.bp)[:, kk * 128:(kk + 1) * 128], in_=pb[s.pi][:, kk * 128:(kk + 1) * 128], identity=ident_b[:])
                    d(ins)
                OP('pe', f_trp, reads=[res(f"pb{s.pi}"), res("ident_b")], writes=[pres[s.bp]])

            @at(2.8)
            def _():
                s.pt = alloc('pTt', [0, 1, 2])
                claim(res(f"pTt{s.pt}"), 2)
                OP('act', lambda e, d: d(e.copy(out=pTt[s.pt][:], in_=bank_bf(s.bp)[:, 0:256].rearrange("p (k c) -> p k c", k=2))),
                   reads=[pres[s.bp]], writes=[res(f"pTt{s.pt}")])

            @at(3.0)
            def _():
                s.q = palloc()

                def f_o(e, d):
                    ins = None
                    for h in range(2):
                        for kk in range(8):
                            ins = e.matmul(ps[s.q][:, h * 512:(h + 1) * 512], lhsT=mT[:, kk, tl.rel:tl.rel + 128],
                                           rhs=wo_sb[:, kk, h * 512:(h + 1) * 512], start=(kk == 0), stop=(kk == 7))
                    d(ins)
                OP('pe', f_o, reads=[res("wo")] + [ares("mT", jj, k) for jj in range(8)], writes=[pres[2 * s.q], pres[2 * s.q + 1]])

            @at(4.0)
            def _():
                s.w = alloc('xt', [0, 1, 2])
                claim(wres(s.w)[0], 1)
                OP('sp', lambda e, d: d(e.dma_start(out=wap(s.w), in_=tl.x)), writes=wres(s.w), dma=f"ldW{s.w}")

            @at(6.8)
            def _():
                s.xb = alloc('x1b', [0, 1])
                claim(res(f"x1b{s.xb}"), 1)
                OP('act', lambda e, d: d(e.activation(out=x1b[s.xb][:], in_=ps[s.q][:, :], func=AF.Square, accum_out=stat(3, gi))),
                   reads=[pres[2 * s.q], pres[2 * s.q + 1], res("stats")], writes=[res(f"x1b{s.xb}"), res(f"ss2_{gi}")])

            @at(8.1)
            def _():
                OP('act', lambda e, d: d(e.activation(out=stat(4, gi), in_=stat(3, gi), func=AF.Identity, bias=epsc[:, 0:1], scale=1.0 / D)),
                   reads=[res(f"ss2_{gi}"), res("mhalf")], writes=[res(f"ms2_{gi}")])

            @at(8.6)
            def _():
                OP('pool', lambda e, d: d(e.tensor_tensor(out=stat(5, gi), in0=stat(4, gi), in1=mhalf[:, 0:1], op=ALU.pow)),
                   reads=[res(f"ms2_{gi}"), res("mhalf")], writes=[res(f"rstd2_{gi}")])

            @at(9.2)
            def _():
                s.ow = alloc('osb', [3, 4, 5])
                claim(wres(s.ow)[0], 5)
                OP('dve', lambda e, d: d(e.scalar_tensor_tensor(out=wap(s.ow), in0=ps[s.q][:, :], scalar=stat(5, gi), in1=gpost_bc[:],
                                                                 op0=ALU.mult, op1=ALU.mult)),
                   reads=[pres[2 * s.q], pres[2 * s.q + 1], res(f"rstd2_{gi}"), res("gpost")], writes=wres(s.ow))

            @at(11.0)
            def _():
                OP('dve', lambda e, d: d(e.tensor_tensor(out=wap(s.ow), in0=wap(s.ow), in1=wap(s.w), op=ALU.add)),
                   reads=wres(s.ow) + wres(s.w), writes=wres(s.ow))

            @at(12.4)
            def _():
                OP('act', lambda e, d: d(e.copy(out=x1b[s.xb][:], in_=wap(s.ow))), reads=wres(s.ow), writes=[res(f"x1b{s.xb}")])

            @at(13.8)
            def _():
                s.bt = balloc()

                def f_tr(e, d):
                    ins = None
                    for kk in range(8):
                        ins = e.transpose(out=bank_bf(s.bt)[:, kk * 128:(kk + 1) * 128], in_=x1b[s.xb][:, kk * 128:(kk + 1) * 128], identity=ident_b[:])
                    d(ins)
                OP('pe', f_tr, reads=[res(f"x1b{s.xb}"), res("ident_b")], writes=[pres[s.bt]])

            @at(14.6)
            def _():
                s.xi = alloc('x1T', [0])
                claim(res(f"x1T{s.xi}"), 2)
                OP('dve', lambda e, d: d(e.tensor_copy(out=x1T[s.xi][:], in_=bank_bf(s.bt).rearrange("p (k c) -> p k c", k=8))),
                   reads=[pres[s.bt]], writes=[res(f"x1T{s.xi}")])

            for h in range(2):
                hs = Blk()

                @at(15.5 + 2.2 * h)
                def _(h=h, hs=hs):
                    hs.bg = balloc(1, [5, 6, 7])

                    def f_g(e, d):
                        ins = None
                        for kk in range(8):
                            ins = e.matmul(bank(hs.bg), lhsT=x1T[s.xi][:, kk, :], rhs=wpg_sb[:, kk, h * 512:(h + 1) * 512],
                                           start=(kk == 0), stop=(kk == 7))
                        d(ins)
                    OP('pe', f_g, reads=[res(f"x1T{s.xi}"), res("wpg")], writes=[pres[hs.bg]])

                @at(15.6 + 2.6 * h)
                def _(h=h, hs=hs):
                    hs.be = balloc(1, [5, 6, 7])

                    def f_e(e, d):
                        ins = None
                        for kk in range(2):
                            ins = e.matmul(bank(hs.be), lhsT=pTt[s.pt][:, kk, :], rhs=wple_sb[:, kk, h * 512:(h + 1) * 512],
                                           start=(kk == 0), stop=(kk == 1))
                        d(ins)
                    OP('pe', f_e, reads=[res(f"pTt{s.pt}"), res("wple")], writes=[pres[hs.be]])

                @at(17.9 + 2.2 * h)
                def _(h=h, hs=hs):
                    hs.tg = alloc('sg', [12, 13])
                    claim(tres[hs.tg], 2)
                    OP('act', lambda e, d: d(e.activation(out=tap(hs.tg)[:, 0:512], in_=bank(hs.bg), func=AF.Sigmoid)),
                       reads=[pres[hs.bg]], writes=[tres[hs.tg]])

                @at(18.8 + 2.2 * h)
                def _(h=h, hs=hs):
                    OP('dve', lambda e, d: d(e.tensor_tensor(out=tap(hs.tg)[:, 0:512], in0=bank(hs.be), in1=tap(hs.tg)[:, 0:512], op=ALU.mult)),
                       reads=[pres[hs.be], tres[hs.tg]], writes=[tres[hs.tg]])

                @at(19.8 + 2.2 * h)
                def _(h=h, hs=hs):
                    OP('pool', lambda e, d: d(e.tensor_tensor(out=wap(s.ow)[:, h * 512:(h + 1) * 512], in0=tap(hs.tg)[:, 0:512],
                                                              in1=wap(s.ow)[:, h * 512:(h + 1) * 512], op=ALU.add)),
                       reads=[tres[hs.tg]] + wres(s.ow), writes=wres(s.ow))

            @at(23.6)
            def _():
                OP('sp', lambda e, d: d(e.dma_start(out=tl.y, in_=wap(s.ow))), reads=wres(s.ow), writes=[res(f"y_{tl.gi}")], dma=f"stW{s.ow}")
            return ops

        def final_state_outputs():
            fin = [
                (ncp, 2, lambda j: tail_u[:, j, 0:2], [f"tail_u{j}" for j in range(8)], 0, 0),
                (ncs, 8, lambda j: tail_u[:, j, 2:10], [f"tail_su{j}" for j in range(8)], 1, 1),
                (npp, 15, lambda j: tail_x[:, j, 0:15], [f"tail_x{j}" for j in range(8)], 2, 0),
                (nps, 60, lambda j: tail_x[:, j, 15:75], [f"tail_sx{j}" for j in range(8)], 3, 1),
            ]
            for dst, rows, src, rnames, q, w in fin:
                def f_t(e, d, rows=rows, src=src, q=q):
                    ins = None
                    for j in range(8):
                        ins = e.transpose(out=ps[q][0:rows, j * 128:(j + 1) * 128], in_=src(j), identity=ident_f[:])
                    d(ins)
                OP('pe', f_t, reads=[res(n) for n in rnames] + [res("ident_f")], writes=[pres[2 * q], pres[2 * q + 1]])
                ww = [3, 4, 5, 6][q]
                OP('act', lambda e, d, rows=rows, q=q, ww=ww: d(e.copy(out=wap(ww)[0:rows, :], in_=ps[q][0:rows, :])),
                   reads=[pres[2 * q], pres[2 * q + 1]], writes=wres(ww))
                OP('sp', lambda e, d, rows=rows, ww=ww, dst=dst: d(e.dma_start(out=dst[:, :], in_=wap(ww)[0:rows, :])),
                   reads=wres(ww), writes=[res(f"fin{q}")], dma=f"stW{ww}")


        nsb = len(SBS)
        items = []
        for n, tl in enumerate(sb_info[0].tiles):
            items += phase0_ops(tl, n * P0_PERIOD, xt_idx=(0, 1, 2, 3, 4, 5))
        items += [(0.3, lambda: issue_slab_load('head')), (5.0, lambda: issue_slab_load('rest')), (0.05, setup_gpre), (0.6, setup_stage_load), (3.0, setup_stage_transpose), (30.0, setup_gpost)]
        sl0, slr0 = slabs[0], slab_res[0]
        k00 = sb_info[0].blocks[0]
        t_first = 3 * P0_PERIOD + 7.2 + 0.3
        pe_items, pre_banks0 = phase1_pe_items(k00, sl0, slr0, t_first, 2.1, [0, 1, 2, 3, 4, 5])
        items += pe_items
        bank_state['singles'] = [6, 7]
        run_timed(items)
        for _ in range(N_SLAB - 1):
            issue_slab_load()
        load_wgrp()
        for sbi in range(nsb):
            info = sb_info[sbi]
            bank_state['singles'] = list(range(8))
            for j in range(8):
                sl, slr = slab_of(sbi, 1, j)
                for k in info.blocks:
                    phase1_unit(sbi, info, j, k, sl, slr, pre_banks=pre_banks0 if (sbi, j, k.idx) == (0, 0, 0) else None)
                    flush_b2(keep=(2 if j == 7 else 1) if j % 2 == 1 else 0)
                    if resident_loads and (j, k.idx) != (0, 0):
                        resident_loads.pop(0)()
                issue_slab_load()
            p3_items = []
            for i, tl in enumerate(info.tiles):
                p3_items += phase3_ops(tl, i * P3_PERIOD, pre_t=(-1000.0 + i) if i < 2 else None)
            for j in range(8):
                sl, slr = slab_of(sbi, 2, j)
                if j == 6:
                    run_timed([it for it in p3_items if it[0] < -500.0])
                    p3_items = [it for it in p3_items if it[0] >= -500.0]
                for k in info.blocks:
                    if j == 0 and k.idx == 0:
                        flush_b2(keep=len(info.blocks) - 1)
                    run_timed(phase2_ops(info, j, k, sl, slr, 0.0))
                    if j == 0 and k.idx == 0:
                        flush_b2(0)
                    if resident_loads:
                        resident_loads.pop(0)()
                issue_slab_load()
                if j == 2 and sbi == nsb - 1:
                    final_state_outputs()
            bank_state['singles'] = [4]
            rot['osb'] = 0
            rot['sg'] = 0
            nxt = sb_info[sbi + 1].tiles if sbi + 1 < nsb else []
            items = p3_items
            pair0 = rot.get('pair', 0) % 2
            for n, tl in enumerate(nxt):
                n_eff = n if n < len(info.tiles) else len(info.tiles) - 0.5
                items += phase0_ops(tl, n_eff * P3_PERIOD + 7.5, tr_bank=2 * ((pair0 + n) % 2), load_lead=5.0 if n < 2 else 0.0)
            run_timed(items)

        P.finish('sp')
        P.emit_all(nc, st)
    return nc


_NC_CACHE = {}


def kernel(x_prompt, x_sample, p_prompt, p_sample, cache_conv, state_pool, g_pre, w_in, conv_w, conv_b,
           w_grp, pool_scale, w_a_out, w_b_out, w_o, g_post, w_ple, w_pgate):
    f = lambda a: np.ascontiguousarray(np.asarray(a, dtype=np.float32))
    x_prompt, x_sample, p_prompt, p_sample = f(x_prompt), f(x_sample), f(p_prompt), f(p_sample)
    cache_conv, state_pool = f(cache_conv), f(state_pool)
    shared = {
        "g_pre": f(g_pre).reshape(1, D), "w_in": np.ascontiguousarray(f(w_in).reshape(8, 128, 8, 8, 128).transpose(2, 3, 1, 0, 4)).reshape(64 * 128, D), "conv_w": f(conv_w).reshape(3, D),
        "conv_b": f(conv_b).reshape(1, D), "w_grp": f(w_grp).reshape(1024, 256), "pool_scale": f(pool_scale).reshape(1, D),
        "w_a_out": np.ascontiguousarray(f(w_a_out).reshape(8, 128, 8, 128).transpose(2, 1, 0, 3)).reshape(8 * 128, D), "w_b_out": np.ascontiguousarray(f(w_b_out).reshape(8, 128, 8, 128).transpose(2, 1, 0, 3)).reshape(8 * 128, D), "w_o": f(w_o).reshape(D, D),
        "g_post": f(g_post).reshape(1, D), "w_ple": f(w_ple).reshape(256, D), "w_pgate": f(w_pgate).reshape(D, D),
    }
    in_maps = []
    for c in range(N_CORES):
        m = dict(shared)
        m["xp"] = x_prompt[c]
        m["xs"] = x_sample[4 * c:4 * c + 4].reshape(128, D)
        m["pp"] = p_prompt[0, c]
        m["ps"] = p_sample[0, 4 * c:4 * c + 4].reshape(128, 256)
        m["cc"] = cache_conv[0, 4 * c:4 * c + 4].reshape(8, D)
        m["sp"] = state_pool[0, 4 * c:4 * c + 4].reshape(60, D)
        in_maps.append(m)
    if "nc" not in _NC_CACHE:
        _NC_CACHE["nc"] = build_program()
    nc = _NC_CACHE["nc"]
    res = run_bass_kernel_spmd(nc, in_maps, core_ids=list(range(N_CORES)))
    rs = res.results
    y_prompt = np.stack([rs[c]["yp"] for c in range(N_CORES)]).astype(np.float32)
    y_sample = np.concatenate([rs[c]["ys"].reshape(4, 32, D) for c in range(N_CORES)]).astype(np.float32)
    ncp = np.stack([rs[c]["ncp"] for c in range(N_CORES)])[None].astype(np.float32)
    ncs = np.concatenate([rs[c]["ncs"].reshape(4, 2, D) for c in range(N_CORES)])[None].astype(np.float32)
    npp = np.stack([rs[c]["npp"] for c in range(N_CORES)])[None].astype(np.float32)
    nps = np.concatenate([rs[c]["nps"].reshape(4, 15, D) for c in range(N_CORES)])[None].astype(np.float32)
    return (y_prompt, y_sample, ncp, ncs, npp, nps)
```
